# Optimizing a Trainium2 kernel written in Bass

```python
import jax, jax.numpy as jnp
from jax import lax
import numpy as np

D_MODEL = 2048
BATCH = 4
SEQ = 4096
DEPTH = 1
DEC_BATCH = 4
DEC_SEQ = 2048
PAST_LEN = 128

MIX_WIDTH = D_MODEL
HEAD_DIM = 128
N_HEADS_TOTAL = MIX_WIDTH // HEAD_DIM
N_HEADS_HGRN = N_HEADS_TOTAL // 2
N_HEADS_ATTN = N_HEADS_TOTAL - N_HEADS_HGRN
HGRN_DK = 128
HGRN_DV = HEAD_DIM
HGRN_FWIDTH = N_HEADS_HGRN * HGRN_DK
HGRN_WIDTH = N_HEADS_HGRN * HGRN_DV
ATTN_WIDTH = N_HEADS_ATTN * HEAD_DIM
IN_SPLITS = (HGRN_FWIDTH, HGRN_FWIDTH, HGRN_FWIDTH, HGRN_WIDTH, HGRN_WIDTH,
             ATTN_WIDTH, ATTN_WIDTH, ATTN_WIDTH)
IN_WIDTH = sum(IN_SPLITS)
CHUNK = 64
DILATED_PATTERNS = ((128, 1), (512, 4), (2048, 16))
ATTN_BLOCK = 64
D_FF = 5632
ALPHA = (2 * DEPTH) ** 0.25
BETA_INIT = (8 * DEPTH) ** -0.25
LN_EPS = 1e-5
RMS_EPS = 1e-6
NEG_INF = -1e30

kernel_name = "hybrid_hgrn2_dilated_alibi_encoder"


def layer_norm(x, g, b):
    xf = x.astype(jnp.float32)
    mu = jnp.mean(xf, axis=-1, keepdims=True)
    var = jnp.mean(jnp.square(xf - mu), axis=-1, keepdims=True)
    return ((xf - mu) * lax.rsqrt(var + LN_EPS) * g.astype(jnp.float32) + b.astype(jnp.float32)).astype(x.dtype)


def swiglu(x, w_gate, w_up, w_down):
    return (jax.nn.silu(x @ w_gate) * (x @ w_up)) @ w_down


def hgrn2_direction(q, k, v, logf):
    B, S, H, DK = q.shape
    DV = v.shape[-1]
    n = S // CHUNK

    def chunks(a):
        return a.reshape(B, n, CHUNK, H, a.shape[-1]).transpose(1, 0, 3, 2, 4)

    tri = jnp.tril(jnp.ones((CHUNK, CHUNK), dtype=bool))

    def step(state, inp):
        qc, kc, vc, lf = inp
        G = jnp.cumsum(lf, axis=2)
        inter = jnp.einsum('bhtk,bhkv->bhtv', qc * jnp.exp(G), state)
        diff = G[:, :, :, None, :] - G[:, :, None, :, :]
        decay = jnp.where(tri[:, :, None], jnp.exp(jnp.minimum(diff, 0.0)), 0.0)
        A = jnp.einsum('bhtk,bhsk,bhtsk->bhts', qc, kc, decay)
        intra = jnp.einsum('bhts,bhsv->bhtv', A, vc)
        g_last = G[:, :, -1, :]
        new_state = (jnp.exp(g_last)[..., None] * state
                     + jnp.einsum('bhsk,bhsv->bhkv', kc * jnp.exp(g_last[:, :, None, :] - G), vc))
        return new_state, inter + intra

    state0 = jnp.zeros((B, H, DK, DV), jnp.float32)
    _, o = lax.scan(step, state0, (chunks(q), chunks(k), chunks(v), chunks(logf)))
    return o.transpose(1, 0, 3, 2, 4).reshape(B, S, H, DV)


def dilated_window_attention(q, k, v, slopes, window, dilation):
    B, S, H, hd = q.shape
    half = (window // 2) // dilation
    blk = ATTN_BLOCK
    L = S // dilation
    BD = B * dilation

    def split(a):
        return a.reshape(B, L, dilation, H, hd).transpose(0, 2, 1, 3, 4).reshape(BD, L, H, hd)

    qs, ks, vs = split(q), split(k), split(v)
    nb = -(-L // blk)
    Lp = nb * blk
    pad = Lp - L
    qb = jnp.pad(qs, ((0, 0), (0, pad), (0, 0), (0, 0))).reshape(BD, nb, blk, H, hd)

    def windows(a):
        ap = jnp.pad(a, ((0, 0), (blk, blk + pad), (0, 0), (0, 0))).reshape(BD, nb + 2, blk, H, hd)
        return jnp.concatenate([ap[:, :-2], ap[:, 1:-1], ap[:, 2:]], axis=2)

    kw, vw = windows(ks), windows(vs)
    rel = jnp.arange(3 * blk)[None, :] - blk - jnp.arange(blk)[:, None]
    jpos = jnp.arange(nb)[:, None, None] * blk + rel[None]
    valid = (jpos >= 0) & (jpos < L) & (jnp.abs(rel)[None] <= half)
    dist = (dilation * jnp.abs(rel)).astype(jnp.float32)
    bias = -slopes[:, None, None] * dist[None]

    scores = jnp.einsum('znqhd,znkhd->znhqk', qb, kw,
                        preferred_element_type=jnp.float32) * (hd ** -0.5) + bias[None, None]
    scores = jnp.where(valid[None, :, None], scores, NEG_INF)
    m = jnp.max(scores, axis=-1, keepdims=True)
    p = jnp.exp(scores - m)
    den = jnp.sum(p, axis=-1)
    out = jnp.einsum('znhqk,znkhd->znqhd', p.astype(v.dtype), vw,
                     preferred_element_type=jnp.float32) / den.transpose(0, 1, 3, 2)[..., None]
    lse = (m[..., 0] + jnp.log(den)).transpose(0, 1, 3, 2)

    out = out.reshape(BD, Lp, H, hd)[:, :L]
    lse = lse.reshape(BD, Lp, H)[:, :L]
    out = out.reshape(B, dilation, L, H, hd).transpose(0, 2, 1, 3, 4).reshape(B, S, H, hd)
    lse = lse.reshape(B, dilation, L, H).transpose(0, 2, 1, 3).reshape(B, S, H)
    return out, lse


def hybrid_mixer(h, w_in, lb_fwd, lb_bwd, norm_g, w_out):
    B, S, _ = h.shape
    f32 = jnp.float32
    proj = h @ w_in
    cuts = list(np.cumsum(IN_SPLITS)[:-1])
    q_h, zf_f, zf_b, i_h, g_h, q_a, k_a, v_a = jnp.split(proj, cuts, axis=-1)

    def heads(a, d):
        return a.reshape(B, S, N_HEADS_HGRN, d).astype(f32)

    def gates(z, lb):
        f = lb + (1.0 - lb) * jax.nn.sigmoid(z.astype(f32))
        return heads(jnp.log(f), HGRN_DK), heads(1.0 - f, HGRN_DK)

    qh = heads(q_h, HGRN_DK)
    vh = heads(i_h, HGRN_DV)
    logf_f, k_f = gates(zf_f, lb_fwd)
    logf_b, k_b = gates(zf_b, lb_bwd)
    o_f = hgrn2_direction(qh, k_f, vh, logf_f)
    flip = lambda a: jnp.flip(a, axis=1)
    o_b = flip(hgrn2_direction(flip(qh), flip(k_b), flip(vh), flip(logf_b)))
    o = o_f + o_b
    o = o * lax.rsqrt(jnp.mean(jnp.square(o), axis=-1, keepdims=True) + RMS_EPS)
    hgrn_out = (o.reshape(B, S, HGRN_WIDTH) * norm_g.astype(f32)
                * jax.nn.silu(g_h.astype(f32))).astype(h.dtype)

    qa = q_a.reshape(B, S, N_HEADS_ATTN, HEAD_DIM)
    ka = k_a.reshape(B, S, N_HEADS_ATTN, HEAD_DIM)
    va = v_a.reshape(B, S, N_HEADS_ATTN, HEAD_DIM)
    slopes = 2.0 ** (-8.0 * (jnp.arange(N_HEADS_ATTN, dtype=f32) + 1.0) / N_HEADS_ATTN)
    outs, lses = [], []
    for window, dilation in DILATED_PATTERNS:
        o_p, l_p = dilated_window_attention(qa, ka, va, slopes, window, dilation)
        outs.append(o_p)
        lses.append(l_p)
    wts = jax.nn.softmax(jnp.stack(lses, axis=0), axis=0)
    attn = jnp.sum(wts[..., None] * jnp.stack(outs, axis=0), axis=0)
    attn_out = attn.reshape(B, S, ATTN_WIDTH).astype(h.dtype)

    return jnp.concatenate([hgrn_out, attn_out], axis=-1) @ w_out


def encoder_trunk(x, ln1_g, ln1_b, ffn1_w_gate, ffn1_w_up, ffn1_w_down,
                  ln2_g, ln2_b, w_in, hgrn_lb_fwd, hgrn_lb_bwd, hgrn_norm_g, w_out,
                  ln3_g, ln3_b, ffn2_w_gate, ffn2_w_up, ffn2_w_down):
    lbs_f = jnp.cumsum(jax.nn.softmax(hgrn_lb_fwd.astype(jnp.float32), axis=0), axis=0)
    lbs_b = jnp.cumsum(jax.nn.softmax(hgrn_lb_bwd.astype(jnp.float32), axis=0), axis=0)
    for l in range(DEPTH):
        x = layer_norm(ALPHA * x + 0.5 * swiglu(x, ffn1_w_gate[l], ffn1_w_up[l], ffn1_w_down[l]),
                       ln1_g[l], ln1_b[l])
        x = layer_norm(ALPHA * x + hybrid_mixer(x, w_in[l], lbs_f[l], lbs_b[l], hgrn_norm_g[l], w_out[l]),
                       ln2_g[l], ln2_b[l])
        x = layer_norm(ALPHA * x + 0.5 * swiglu(x, ffn2_w_gate[l], ffn2_w_up[l], ffn2_w_down[l]),
                       ln3_g[l], ln3_b[l])
    return x


def setup_inputs(seed: int = 0) -> dict:
    key = jax.random.key(seed)
    ks = jax.random.split(key, 24)
    f32 = jnp.float32
    nrm = lambda k, shape, s: jax.random.normal(k, shape, f32) * s
    gain = lambda k: 1.0 + nrm(k, (DEPTH, D_MODEL), 0.02)
    bias = lambda k: nrm(k, (DEPTH, D_MODEL), 0.02)
    return {
        "x_prompt": nrm(ks[0], (BATCH, SEQ, D_MODEL), 1.0),
        "x_sample": nrm(ks[1], (DEC_BATCH, DEC_SEQ, D_MODEL), 1.0),
        "ln1_g": gain(ks[2]),
        "ln1_b": bias(ks[3]),
        "ffn1_w_gate": nrm(ks[4], (DEPTH, D_MODEL, D_FF), D_MODEL ** -0.5),
        "ffn1_w_up": nrm(ks[5], (DEPTH, D_MODEL, D_FF), D_MODEL ** -0.5),
        "ffn1_w_down": nrm(ks[6], (DEPTH, D_FF, D_MODEL), BETA_INIT * D_FF ** -0.5),
        "ln2_g": gain(ks[7]),
        "ln2_b": bias(ks[8]),
        "w_in": nrm(ks[9], (DEPTH, D_MODEL, IN_WIDTH), D_MODEL ** -0.5),
        "hgrn_lb_fwd": nrm(ks[10], (DEPTH + 1, HGRN_FWIDTH), 0.5),
        "hgrn_lb_bwd": nrm(ks[11], (DEPTH + 1, HGRN_FWIDTH), 0.5),
        "hgrn_norm_g": 1.0 + nrm(ks[12], (DEPTH, HGRN_WIDTH), 0.02),
        "w_out": nrm(ks[13], (DEPTH, MIX_WIDTH, D_MODEL), BETA_INIT * MIX_WIDTH ** -0.5),
        "ln3_g": gain(ks[14]),
        "ln3_b": bias(ks[15]),
        "ffn2_w_gate": nrm(ks[16], (DEPTH, D_MODEL, D_FF), D_MODEL ** -0.5),
        "ffn2_w_up": nrm(ks[17], (DEPTH, D_MODEL, D_FF), D_MODEL ** -0.5),
        "ffn2_w_down": nrm(ks[18], (DEPTH, D_FF, D_MODEL), BETA_INIT * D_FF ** -0.5),
    }


def reference(x_prompt, x_sample, ln1_g, ln1_b, ffn1_w_gate, ffn1_w_up, ffn1_w_down,
              ln2_g, ln2_b, w_in, hgrn_lb_fwd, hgrn_lb_bwd, hgrn_norm_g, w_out,
              ln3_g, ln3_b, ffn2_w_gate, ffn2_w_up, ffn2_w_down):
    y_prompt = encoder_trunk(x_prompt, ln1_g, ln1_b, ffn1_w_gate, ffn1_w_up, ffn1_w_down,
                             ln2_g, ln2_b, w_in, hgrn_lb_fwd, hgrn_lb_bwd, hgrn_norm_g, w_out,
                             ln3_g, ln3_b, ffn2_w_gate, ffn2_w_up, ffn2_w_down)
    y_sample = encoder_trunk(x_sample, ln1_g, ln1_b, ffn1_w_gate, ffn1_w_up, ffn1_w_down,
                             ln2_g, ln2_b, w_in, hgrn_lb_fwd, hgrn_lb_bwd, hgrn_norm_g, w_out,
                             ln3_g, ln3_b, ffn2_w_gate, ffn2_w_up, ffn2_w_down)
    return (y_prompt, y_sample)
```

```python
import numpy as np
from contextlib import ExitStack
import concourse.bass as bass
import concourse.mybir as mybir
from concourse.bass_utils import run_bass_kernel_spmd

F32 = mybir.dt.float32
BF16 = mybir.dt.bfloat16
AF = mybir.ActivationFunctionType
ALU = mybir.AluOpType

D = 2048
DFF = 5632
NKC = D // 128
NFC = DFF // 128
HD = 128
NH = 8
NHR = NH
ALPHA = 2.0 ** 0.25
LN_EPS = 1e-5
RMS_EPS = 1e-6
PATTERNS = (1, 4, 16)
TT = 512
NEG = -30000.0


_UID = [0]


def _sb(nc, name, shape, dtype):
    _UID[0] += 1
    return nc.sbuf_tensor("sb%d_%s" % (_UID[0], name), shape, dtype)


def _ss(start, n, step):
    return slice(start, start + (n - 1) * step + 1, step) if step > 1 else slice(start, start + n)


def _dsplit(eng, out3, in3, n, step=8):
    res = []
    for a in range(0, n, step):
        b = min(n, a + step)
        res.append(eng.dma_start(out=out3[:, a:b, :], in_=in3[:, a:b, :]))
    return res


class _Op:
    __slots__ = ("eng", "sig", "dma")


class Sched:
    def __init__(self, nc, es):
        self.nc = nc
        self.es = es
        self.engs = {"pe": nc.tensor, "act": nc.scalar, "dve": nc.vector, "pool": nc.gpsimd, "sp": nc.sync}
        self.esem = {k: es.enter_context(nc.semaphore("s_" + k)) for k in self.engs}
        self.ecnt = {k: 0 for k in self.engs}
        self.dsem = {}
        self.dcnt = {}
        self.waited = {k: {} for k in self.engs}
        self.last_w = {}
        self.readers = {}
        self.nops = 0

    def op(self, eng, fn, reads=(), writes=(), dkey=None):
        E = self.engs[eng]
        deps = []
        for r in reads:
            w = self.last_w.get(r)
            if w is not None:
                deps.append(w)
        for r in writes:
            w = self.last_w.get(r)
            if w is not None:
                deps.append(w)
            deps.extend(self.readers.get(r, ()))
        wt = self.waited[eng]
        for d in deps:
            if d.eng == eng and eng == "pe" and not d.dma:
                continue
            sem, val = d.sig
            key = id(sem)
            if wt.get(key, 0) < val:
                E.wait_ge(sem, val)
                wt[key] = val
        insts = fn()
        o = _Op()
        o.eng = eng
        if dkey is None:
            inst = insts[-1] if isinstance(insts, (list, tuple)) else insts
            sem = self.esem[eng]
            self.ecnt[eng] += 1
            inst.then_inc(sem, 1)
            o.sig = (sem, self.ecnt[eng])
            o.dma = False
        else:
            dkey = (dkey, eng)
            if dkey not in self.dsem:
                self.dsem[dkey] = self.es.enter_context(self.nc.semaphore("d%d" % len(self.dsem)))
                self.dcnt[dkey] = 0
            sem = self.dsem[dkey]
            if not isinstance(insts, (list, tuple)):
                insts = [insts]
            for inst in insts:
                inst.then_inc(sem, 16)
                self.dcnt[dkey] += 16
            o.sig = (sem, self.dcnt[dkey])
            o.dma = True
        for r in reads:
            self.readers.setdefault(r, []).append(o)
        for r in writes:
            self.last_w[r] = o
            self.readers[r] = []
        self.nops += 1
        return o

    def barrier(self):
        for k, E in self.engs.items():
            wt = self.waited[k]
            for k2, sem in self.esem.items():
                if k2 != k and self.ecnt[k2] > 0 and wt.get(id(sem), 0) < self.ecnt[k2]:
                    E.wait_ge(sem, self.ecnt[k2])
                    wt[id(sem)] = self.ecnt[k2]
            for dk, sem in self.dsem.items():
                if self.dcnt[dk] > 0 and wt.get(id(sem), 0) < self.dcnt[dk]:
                    E.wait_ge(sem, self.dcnt[dk])
                    wt[id(sem)] = self.dcnt[dk]

    def finish(self):
        sp = self.nc.sync
        for k, sem in self.dsem.items():
            if self.dcnt[k] > 0:
                sp.wait_ge(sem, self.dcnt[k])
        for k, sem in self.esem.items():
            if k != "sp" and self.ecnt[k] > 0:
                sp.wait_ge(sem, self.ecnt[k])


def build_program(T, debug=False):
    assert T % 2048 == 0
    NT = T // TT
    NC = T // 64
    nc = bass.Bass("TRN2", target_bir_lowering=False)
    dt = nc.dram_tensor
    kin = "ExternalInput"
    xT = dt("xT", [D, T], F32, kind=kin).ap()
    w_src = {
        "g1": dt("wg1", [D, DFF], F32, kind=kin).ap(), "u1": dt("wu1", [D, DFF], F32, kind=kin).ap(),
        "d1": dt("wd1", [DFF, D], F32, kind=kin).ap(),
        "g2": dt("wg2", [D, DFF], F32, kind=kin).ap(), "u2": dt("wu2", [D, DFF], F32, kind=kin).ap(),
        "d2": dt("wd2", [DFF, D], F32, kind=kin).ap(),
        "in": dt("w_in", [D, 8192], F32, kind=kin).ap(), "out": dt("w_out", [D, D], F32, kind=kin).ap(),
    }
    lnp_d = dt("lnp", [128, 6 * NKC], F32, kind=kin).ap()
    hgp_d = dt("hgp", [128, 5 * NH], F32, kind=kin).ap()
    cont_d = dt("cont", [128, 1], F32, kind=kin).ap()
    abias_d = dt("abias", [128, 24 * 256], F32, kind=kin).ap()
    ebias_d = dt("ebias", [128, 24 * 256], F32, kind=kin).ap()
    cmask_d = dt("cmask", [128, 2 * 128 + 2048], F32, kind=kin).ap()
    yT = dt("yT", [D, T], F32, kind="ExternalOutput").ap()
    okind = "ExternalOutput" if debug else "Internal"
    X1T = dt("X1T", [D, T], F32, kind=okind).ap()
    PF = dt("PF", [4096, T], F32, kind=okind).ap()
    PA = dt("PA", [2048, T], BF16, kind="Internal").ap()
    VT = dt("VT", [T, 2048], BF16, kind="Internal").ap()
    MIX = dt("MIX", [2048, T], BF16, kind=okind).ap()
    WS = {
        "gu1": dt("ws_gu1", [22, 128, 16 * 512], BF16).ap(), "gu2": dt("ws_gu2", [22, 128, 16 * 512], BF16).ap(),
        "d1": dt("ws_d1", [16, 128, NFC * 128], BF16).ap(), "d2": dt("ws_d2", [16, 128, NFC * 128], BF16).ap(),
        "in": dt("ws_in", [16, 128, 16 * 512], BF16).ap(), "out": dt("ws_out", [4, 128, 16 * 512], BF16).ap(),
    }

    with ExitStack() as es:
        E = es.enter_context
        S = Sched(nc, es)
        op = S.op
        ps = [E(nc.psum_tensor("ps%d" % i, [128, 512], F32)) for i in range(8)]

        lnp = E(_sb(nc, "lnp", [128, 6, NKC], F32))
        hgp = E(_sb(nc, "hgp", [128, 5, NH], F32))
        cont = E(_sb(nc, "cont", [128, 1], F32))
        gat = E(_sb(nc, "gat", [128, 6, NH], F32))
        ngs = E(_sb(nc, "ngs", [128, NH], F32))
        ones32 = E(_sb(nc, "ones32", [128, 128], F32))
        onesb = E(_sb(nc, "onesb", [128, 128], BF16))
        ident = E(_sb(nc, "ident", [128, 128], BF16))
        cm32 = E(_sb(nc, "cm32", [128, 2 * 128 + 2048], F32))
        maskFB = E(_sb(nc, "maskFB", [128, 2, 128], F32))

        op("sp", lambda: nc.sync.dma_start(out=lnp[:].rearrange("p a c -> p (a c)"), in_=lnp_d), writes=["lnp"], dkey="c0")
        op("sp", lambda: nc.sync.dma_start(out=hgp[:].rearrange("p a c -> p (a c)"), in_=hgp_d), writes=["hgp"], dkey="c1")
        op("sp", lambda: nc.sync.dma_start(out=cont[:], in_=cont_d), writes=["cont"], dkey="c2")
        op("sp", lambda: nc.sync.dma_start(out=cm32[:], in_=cmask_d), writes=["cm32"], dkey="c3")
        epst = E(_sb(nc, "epst", [128, 3], F32))
        op("dve", lambda: nc.vector.memset(epst[:, 0:1], LN_EPS), writes=["epst"])
        op("dve", lambda: nc.vector.memset(epst[:, 1:2], 4.0 * LN_EPS), writes=["epst"])
        op("dve", lambda: nc.vector.memset(epst[:, 2:3], 128.0 * RMS_EPS), writes=["epst"])
        op("dve", lambda: nc.vector.memset(ones32[:], 1.0), writes=["ones32"])
        op("dve", lambda: nc.vector.memset(onesb[:], 1.0), writes=["onesb"])
        op("dve", lambda: nc.vector.tensor_copy(out=maskFB[:].rearrange("p a c -> p (a c)"), in_=cm32[:, 0:256]),
           reads=["cm32"], writes=["maskFB"])
        tmpi = E(_sb(nc, "tmpi", [128, 128], F32))
        op("dve", lambda: nc.vector.tensor_tensor(out=tmpi[:], in0=cm32[:, 0:128], in1=cm32[:, 128:256], op=ALU.mult),
           reads=["cm32"], writes=["tmpi"])
        op("dve", lambda: nc.vector.tensor_copy(out=ident[:], in_=tmpi[:]), reads=["tmpi"], writes=["ident"])
        for di in range(2):
            op("dve", lambda di=di: nc.vector.tensor_tensor(out=gat[:, 3 * di, :], in0=hgp[:, 2 * di, :], in1=hgp[:, 2 * di + 1, :],
                                                          op=ALU.subtract), reads=["hgp"], writes=["gat"])
            op("act", lambda di=di: nc.scalar.activation(out=gat[:, 3 * di, :], in_=gat[:, 3 * di, :], func=AF.Sigmoid),
               reads=["gat"], writes=["gat"])
            op("dve", lambda di=di: nc.vector.tensor_scalar(out=gat[:, 3 * di + 1, :], in0=gat[:, 3 * di, :], scalar1=-1.0, scalar2=1.0,
                                                          op0=ALU.mult, op1=ALU.add), reads=["gat"], writes=["gat"])
            op("dve", lambda di=di: nc.vector.tensor_scalar(out=gat[:, 3 * di + 2, :], in0=gat[:, 3 * di + 1, :], scalar1=-1.0, scalar2=None,
                                                          op0=ALU.mult), reads=["gat"], writes=["gat"])
        op("dve", lambda: nc.vector.tensor_scalar(out=ngs[:], in0=hgp[:, 4, :], scalar1=float(128.0 ** 0.5), scalar2=None, op0=ALU.mult),
           reads=["hgp"], writes=["ngs"])

        def row_phase(which):
          with ExitStack() as es2:
            E2 = es2.enter_context
            R = E2(_sb(nc, "R", [128, NKC, TT], F32))
            xbf = E2(_sb(nc, "xbf", [128, NKC, TT], BF16))
            aT = E2(_sb(nc, "aT", [128, NFC, TT], BF16))
            wring = [E2(_sb(nc, "wr%d" % i, [128, 16 * 512], BF16)) for i in range(3)]
            wdring = [E2(_sb(nc, "wd%d" % i, [128, NFC * 128], BF16)) for i in range(2)]
            sil = [E2(_sb(nc, "sil%d" % i, [128, TT], F32)) for i in range(2)]
            sq = [E2(_sb(nc, "sq%d" % i, [128, TT], F32)) for i in range(2)]
            acc1 = E2(_sb(nc, "acc1", [128, TT], F32))
            acc2 = E2(_sb(nc, "acc2", [128, TT], F32))
            mean = E2(_sb(nc, "mean", [128, TT], F32))
            m2 = E2(_sb(nc, "m2", [128, TT], F32))
            lnA = E2(_sb(nc, "lnA", [128, TT], F32))
            lnB = E2(_sb(nc, "lnB", [128, TT], F32))
            stg32 = [E2(_sb(nc, "stg32_%d" % i, [128, TT], F32)) for i in range(2)]
            stg16 = [E2(_sb(nc, "stg16_%d" % i, [128, TT], BF16)) for i in range(2)]
            cnt = {"wr": 0, "wd": 0, "stg32": 0, "stg16": 0, "ev": 0}

            def wblock(kind, first, j, srcs):
                slot = cnt["wr"] % 3
                cnt["wr"] += 1
                buf = wring[slot]
                b3 = buf[:].rearrange("p (k n) -> p k n", k=16)
                if first:
                    def ld():
                        r_ = []
                        for (c0, n, src) in srcs:
                            r_ += _dsplit(nc.gpsimd, b3[:, :, c0:c0 + n], src, 16)
                        return r_
                    op("pool", ld, writes=[("wr", slot)], dkey=("wr", slot))
                    op("sp", lambda: nc.sync.dma_start(out=WS[kind][j], in_=buf[:]), reads=[("wr", slot)],
                       writes=[("ws", kind, j)], dkey=("wrs", slot))
                else:
                    op("sp", lambda: nc.sync.dma_start(out=buf[:], in_=WS[kind][j]), reads=[("ws", kind, j)],
                       writes=[("wr", slot)], dkey=("wr", slot))
                return b3, ("wr", slot)

            def wdblock(kind, first, dc, src):
                slot = cnt["wd"] % 2
                cnt["wd"] += 1
                buf = wdring[slot]
                b3 = buf[:].rearrange("p (k n) -> p k n", k=NFC)
                if first:
                    op("pool", lambda: _dsplit(nc.gpsimd, b3, src, NFC), writes=[("wd", slot)], dkey=("wd", slot))
                    op("sp", lambda: nc.sync.dma_start(out=WS[kind][dc], in_=buf[:]), reads=[("wd", slot)],
                       writes=[("ws", kind, dc)], dkey=("wds", slot))
                else:
                    op("sp", lambda: nc.sync.dma_start(out=buf[:], in_=WS[kind][dc]), reads=[("ws", kind, dc)],
                       writes=[("wd", slot)], dkey=("wd", slot))
                return b3, ("wd", slot)

            def mm_group(out_ap, pairs):
                def f():
                    last = None
                    n = len(pairs)
                    for i, (l, r) in enumerate(pairs):
                        last = nc.tensor.matmul(out_ap, lhsT=l, rhs=r, start=(i == 0), stop=(i == n - 1))
                    return last
                return f

            def mm_group_split(out_ap, pairs, wtok, ktoks, btok):
                n = len(pairs)
                for i, (l, r) in enumerate(pairs):
                    op("pe", lambda l=l, r=r, i=i: nc.tensor.matmul(out_ap, lhsT=l, rhs=r, start=(i == 0), stop=(i == n - 1)),
                       reads=[wtok, ktoks[i]], writes=[btok])

            def r_load(dram3, t0, extra_reads, key):
                for g in range(4):
                    ks = slice(4 * g, 4 * g + 4)
                    op("pool", lambda ks=ks: nc.gpsimd.dma_start(out=R[:, ks, :], in_=dram3[:, ks, t0:t0 + TT]),
                       reads=extra_reads, writes=[("R", k) for k in range(4 * g, 4 * g + 4)], dkey=(key, g))

            def r_store(dram3, t0, wtok, key):
                for g in range(4):
                    ks = slice(4 * g, 4 * g + 4)
                    op("pool", lambda ks=ks: nc.gpsimd.dma_start(out=dram3[:, ks, t0:t0 + TT], in_=R[:, ks, :]),
                       reads=[("R", k) for k in range(4 * g, 4 * g + 4)], writes=[wtok], dkey=(key, g))

            def preconvert(items):
                for (kind, j) in items:
                    if kind == "gu2":
                        dst = WS[kind][j].rearrange("p (k n) -> p k n", k=16)
                        g3 = w_src["g2"].rearrange("(k p) n -> p k n", p=128)[:, :, j * 256:(j + 1) * 256]
                        u3 = w_src["u2"].rearrange("(k p) n -> p k n", p=128)[:, :, j * 256:(j + 1) * 256]
                        f = lambda dst=dst, g3=g3, u3=u3: _dsplit(nc.gpsimd, dst[:, :, 0:256], g3, 16) + _dsplit(nc.gpsimd, dst[:, :, 256:512], u3, 16)
                    elif kind == "d2":
                        dst = WS[kind][j].rearrange("p (k n) -> p k n", k=NFC)
                        s3 = w_src["d2"].rearrange("(k p) n -> p k n", p=128)[:, :, j * 128:(j + 1) * 128]
                        f = lambda dst=dst, s3=s3: _dsplit(nc.gpsimd, dst, s3, NFC)
                    else:
                        dst = WS[kind][j].rearrange("p (k n) -> p k n", k=16)
                        s3 = w_src["out"].rearrange("(k p) n -> p k n", p=128)[:, :, j * 512:(j + 1) * 512]
                        f = lambda dst=dst, s3=s3: _dsplit(nc.gpsimd, dst, s3, 16)
                    op("pool", f, writes=[("ws", kind, j)], dkey="cv")

            def epilogue(dc, by, cmul):
                op("dve", lambda: nc.vector.scalar_tensor_tensor(out=R[:, dc, :], in0=R[:, dc, :], scalar=float(cmul), in1=ps[by][:],
                                                                 op0=ALU.mult, op1=ALU.add),
                   reads=[("ps", by), ("R", dc)], writes=[("R", dc)])
                if dc == 0:
                    op("dve", lambda: nc.vector.tensor_copy(out=acc1[:], in_=R[:, 0, :]), reads=[("R", 0)], writes=["acc1"])
                    op("act", lambda: nc.scalar.activation(out=acc2[:], in_=R[:, 0, :], func=AF.Square), reads=[("R", 0)], writes=["acc2"])
                else:
                    s = dc % 2
                    op("dve", lambda: nc.vector.tensor_tensor(out=acc1[:], in0=acc1[:], in1=R[:, dc, :], op=ALU.add),
                       reads=[("R", dc), "acc1"], writes=["acc1"])
                    op("act", lambda: nc.scalar.activation(out=sq[s][:], in_=R[:, dc, :], func=AF.Square), reads=[("R", dc)], writes=[("sq", s)])
                    op("dve", lambda: nc.vector.tensor_tensor(out=acc2[:], in0=acc2[:], in1=sq[s][:], op=ALU.add),
                       reads=[("sq", s), "acc2"], writes=["acc2"])

            def layer_norm(li, eps, final=None):
                op("pe", lambda: nc.tensor.matmul(ps[6][:], lhsT=ones32[:], rhs=acc1[:], start=True, stop=True),
                   reads=["ones32", "acc1"], writes=[("ps", 6)])
                op("pe", lambda: nc.tensor.matmul(ps[7][:], lhsT=ones32[:], rhs=acc2[:], start=True, stop=True),
                   reads=["ones32", "acc2"], writes=[("ps", 7)])
                op("dve", lambda: nc.vector.tensor_scalar(out=mean[:], in0=ps[6][:], scalar1=1.0 / D, scalar2=None, op0=ALU.mult),
                   reads=[("ps", 6)], writes=["mean"])
                op("dve", lambda: nc.vector.tensor_tensor(out=m2[:], in0=mean[:], in1=mean[:], op=ALU.mult), reads=["mean"], writes=["m2"])
                op("dve", lambda: nc.vector.scalar_tensor_tensor(out=m2[:], in0=ps[7][:], scalar=1.0 / D, in1=m2[:], op0=ALU.mult,
                                                                 op1=ALU.subtract), reads=[("ps", 7), "m2"], writes=["m2"])
                ei = {LN_EPS: 0, 4.0 * LN_EPS: 1}[eps]
                op("act", lambda: nc.scalar.activation(out=m2[:], in_=m2[:], func=AF.Sqrt, bias=epst[:, ei:ei + 1], scale=1.0),
                   reads=["m2", "epst"], writes=["m2"])
                op("dve", lambda: nc.vector.reciprocal(out=lnA[:], in_=m2[:]), reads=["m2"], writes=["lnA"])
                for dc in range(NKC):
                    s = dc % 2
                    op("dve", lambda dc=dc, s=s: nc.vector.tensor_tensor(out=sil[s][:], in0=R[:, dc, :], in1=mean[:], op=ALU.subtract),
                       reads=[("R", dc), "mean"], writes=[("sil", s)])
                    op("dve", lambda dc=dc, s=s: nc.vector.tensor_tensor(out=sil[s][:], in0=sil[s][:], in1=lnA[:], op=ALU.mult),
                       reads=[("sil", s), "lnA"], writes=[("sil", s)])
                    if final is not None:
                        t0f, nxt_fn = final
                        so = cnt["stg32"] % 2
                        cnt["stg32"] += 1
                        op("act", lambda dc=dc, s=s, so=so: nc.scalar.activation(out=stg32[so][:], in_=sil[s][:], func=AF.Identity,
                                                                               scale=lnp[:, 2 * li, dc:dc + 1], bias=lnp[:, 2 * li + 1, dc:dc + 1]),
                           reads=[("sil", s), "lnp"], writes=[("stg32", so)])
                        op("pool", lambda dc=dc, so=so: nc.gpsimd.dma_start(out=yT[dc * 128:(dc + 1) * 128, t0f:t0f + TT], in_=stg32[so][:]),
                           reads=[("stg32", so)], writes=[("yT", dc)], dkey=("st32", so))
                        if nxt_fn is not None:
                            nxt_fn(dc)
                        continue
                    op("act", lambda dc=dc, s=s: nc.scalar.activation(out=R[:, dc, :], in_=sil[s][:], func=AF.Identity,
                                                                      scale=lnp[:, 2 * li, dc:dc + 1], bias=lnp[:, 2 * li + 1, dc:dc + 1]),
                       reads=[("sil", s), "lnp"], writes=[("R", dc)])
                    if li != 2:
                        op("act", lambda dc=dc, s=s: nc.scalar.activation(out=xbf[:, dc, :], in_=sil[s][:], func=AF.Identity,
                                                                          scale=lnp[:, 2 * li, dc:dc + 1], bias=lnp[:, 2 * li + 1, dc:dc + 1]),
                           reads=[("sil", s), "lnp"], writes=[("xbf", dc)])

            def ffn(fid, first):
                wg, wu, wd = w_src["g" + fid], w_src["u" + fid], w_src["d" + fid]
                wg3 = wg.rearrange("(k p) n -> p k n", p=128)
                wu3 = wu.rearrange("(k p) n -> p k n", p=128)
                wd3 = wd.rearrange("(k p) n -> p k n", p=128)
                xall = [("xbf", k) for k in range(NKC)]
                for j in range(22):
                    b3, tok = wblock("gu" + fid, first, j, [(0, 256, wg3[:, :, j * 256:(j + 1) * 256]), (256, 256, wu3[:, :, j * 256:(j + 1) * 256])])
                    for c in range(2):
                        fc = 2 * j + c
                        bg, bu, s = fc % 2, 2 + fc % 2, fc % 2
                        if fc == 0:
                            mm_group_split(ps[bg][:], [(b3[:, k, c * 128:(c + 1) * 128], xbf[:, k, :]) for k in range(NKC)], tok, xall, ("ps", bg))
                        else:
                            op("pe", mm_group(ps[bg][:], [(b3[:, k, c * 128:(c + 1) * 128], xbf[:, k, :]) for k in range(NKC)]),
                               reads=[tok] + xall, writes=[("ps", bg)])
                        op("pe", mm_group(ps[bu][:], [(b3[:, k, 256 + c * 128:256 + (c + 1) * 128], xbf[:, k, :]) for k in range(NKC)]),
                           reads=[tok] + xall, writes=[("ps", bu)])
                        op("act", lambda bg=bg, s=s: nc.scalar.activation(out=sil[s][:], in_=ps[bg][:], func=AF.Silu),
                           reads=[("ps", bg)], writes=[("sil", s)])
                        op("dve", lambda bu=bu, s=s, fc=fc: nc.vector.tensor_tensor(out=aT[:, fc, :], in0=sil[s][:], in1=ps[bu][:], op=ALU.mult),
                           reads=[("sil", s), ("ps", bu)], writes=[("a", fc)])
                aall = [("a", f) for f in range(NFC)]
                for dc in range(NKC):
                    b3, tok = wdblock("d" + fid, first, dc, wd3[:, :, dc * 128:(dc + 1) * 128])
                    by = 4 + dc % 2
                    if dc == 0:
                        mm_group_split(ps[by][:], [(b3[:, f, :], aT[:, f, :]) for f in range(NFC)], tok, aall, ("ps", by))
                    else:
                        op("pe", mm_group(ps[by][:], [(b3[:, f, :], aT[:, f, :]) for f in range(NFC)]), reads=[tok] + aall, writes=[("ps", by)])
                    epilogue(dc, by, 2.0 * ALPHA)

            def evac(by, dst, use_act):
                if use_act:
                    return op("act", lambda: nc.scalar.copy(out=dst, in_=ps[by][:]), reads=[("ps", by)], writes=[dst_tok[0]])
                return op("dve", lambda: nc.vector.tensor_copy(out=dst, in_=ps[by][:]), reads=[("ps", by)], writes=[dst_tok[0]])

            dst_tok = [None]
            win3 = w_src["in"].rearrange("(k p) n -> p k n", p=128)
            wout3 = w_src["out"].rearrange("(k p) n -> p k n", p=128)

            conv_items = [("gu2", j) for j in range(22)] + [("d2", j) for j in range(16)] + [("out", j) for j in range(4)]
            for t in (range(NT) if which == "A" else []):
                first = (t == 0)
                t0 = t * TT
                Rall = [("R", k) for k in range(NKC)]
                xT3 = xT.rearrange("(k p) n -> p k n", p=128)
                if t == 0:
                    r_load(xT3, 0, [], "ldx")
                else:
                    nper = (len(conv_items) + NT - 2) // (NT - 1)
                    preconvert(conv_items[(t - 1) * nper:t * nper])
                for k in range(NKC):
                    if k % 2:
                        op("act", lambda k=k: nc.scalar.copy(out=xbf[:, k, :], in_=R[:, k, :]), reads=[("R", k)], writes=[("xbf", k)])
                    else:
                        op("dve", lambda k=k: nc.vector.tensor_copy(out=xbf[:, k, :], in_=R[:, k, :]), reads=[("R", k)], writes=[("xbf", k)])
                ffn("1", first)
                layer_norm(0, 4.0 * LN_EPS)
                r_store(X1T.rearrange("(k p) n -> p k n", p=128), t0, ("X1T", t), "stx1")
                if t + 1 < NT:
                    r_load(xT3, t0 + TT, [], "ldx")
                xall = [("xbf", k) for k in range(NKC)]
                for j in range(16):
                    b3, tok = wblock("in", first, j, [(0, 512, win3[:, :, j * 512:(j + 1) * 512])])
                    tokmajor = j in (6, 7, 14, 15)
                    for c in range(4):
                        by = 4 + cnt["ev"] % 2
                        cnt["ev"] += 1
                        if not tokmajor:
                            if j == 0 and c == 0:
                                mm_group_split(ps[by][:], [(b3[:, k, c * 128:(c + 1) * 128], xbf[:, k, :]) for k in range(NKC)], tok, xall, ("ps", by))
                            else:
                                op("pe", mm_group(ps[by][:], [(b3[:, k, c * 128:(c + 1) * 128], xbf[:, k, :]) for k in range(NKC)]),
                                   reads=[tok] + xall, writes=[("ps", by)])
                            f0 = j * 512 + c * 128
                            if j < 6 or j in (8, 9):
                                row = f0 if j < 6 else 3072 + (f0 - 4096)
                                s = cnt["stg32"] % 2
                                cnt["stg32"] += 1
                                dst_tok[0] = ("stg32", s)
                                evac(by, stg32[s][:], c % 2 == 0)
                                op("pool", lambda s=s, row=row: nc.gpsimd.dma_start(out=PF[row:row + 128, t0:t0 + TT], in_=stg32[s][:]),
                                   reads=[("stg32", s)], writes=[("PF", row // 128, t)], dkey=("st32", s))
                            else:
                                row = f0 - 5120
                                s = cnt["stg16"] % 2
                                cnt["stg16"] += 1
                                dst_tok[0] = ("stg16", s)
                                evac(by, stg16[s][:], c % 2 == 0)
                                op("pool", lambda s=s, row=row: nc.gpsimd.dma_start(out=PA[row:row + 128, t0:t0 + TT], in_=stg16[s][:]),
                                   reads=[("stg16", s)], writes=[("PA", row // 128, t)], dkey=("st16", s))
                        else:
                            op("pe", mm_group(ps[by][:], [(xbf[:, k, c * 128:(c + 1) * 128], b3[:, k, :]) for k in range(NKC)]),
                               reads=[tok] + xall, writes=[("ps", by)])
                            col = (j - 6) * 512 if j < 8 else 1024 + (j - 14) * 512
                            s = cnt["stg16"] % 2
                            cnt["stg16"] += 1
                            dst_tok[0] = ("stg16", s)
                            evac(by, stg16[s][:], c % 2 == 0)
                            op("pool", lambda s=s, col=col, c=c: nc.gpsimd.dma_start(out=VT[t0 + c * 128:t0 + (c + 1) * 128, col:col + 512],
                                                                                      in_=stg16[s][:]),
                               reads=[("stg16", s)], writes=[("VT", t)], dkey=("st16", s))

            for t in (range(NT) if which == "C" else []):
                first = False
                t0 = t * TT
                Rall = [("R", k) for k in range(NKC)]
                mixall = [("a", k) for k in range(NKC)]
                X13 = X1T.rearrange("(k p) n -> p k n", p=128)
                MIX3 = MIX.rearrange("(k p) n -> p k n", p=128)
                if t == 0:
                    op("pool", lambda: _dsplit(nc.gpsimd, aT[:, 0:NKC, :], MIX3[:, :, 0:TT], NKC), reads=["MIX"], writes=mixall, dkey="ldmix")
                    r_load(X13, 0, [("X1T", 0)], "ldx1")
                for j in range(4):
                    b3, tok = wblock("out", first, j, [(0, 512, wout3[:, :, j * 512:(j + 1) * 512])])
                    for c in range(4):
                        dc = j * 4 + c
                        by = dc % 6
                        op("pe", mm_group(ps[by][:], [(b3[:, k, c * 128:(c + 1) * 128], aT[:, k, :]) for k in range(NKC)]),
                           reads=[tok] + mixall, writes=[("ps", by)])
                        epilogue(dc, by, ALPHA)
                layer_norm(1, LN_EPS)
                ffn("2", first)
                nxt_fn = None
                if t + 1 < NT:
                    t1n = t0 + TT
                    op("pool", lambda: _dsplit(nc.gpsimd, aT[:, 0:NKC, :], MIX3[:, :, t1n:t1n + TT], NKC), reads=["MIX"], writes=mixall, dkey="ldmix")
                    def nxt_fn(dc, t1n=t1n, t=t):
                        op("pool", lambda: nc.gpsimd.dma_start(out=R[:, dc, :], in_=X13[:, dc, t1n:t1n + TT]),
                           reads=[("X1T", t + 1)], writes=[("R", dc)], dkey=("ldx1c", dc))
                layer_norm(2, 4.0 * LN_EPS, final=(t0, nxt_fn))
        row_phase("A")
        S.barrier()
        mixer_phase(nc, S, T, ps, PF, PA, VT, MIX, gat, ngs, cont, onesb, ones32, ident, maskFB, cm32, abias_d, epst, ebias_d)
        S.barrier()
        row_phase("C")
        S.finish()
    return nc


def mixer_phase(nc, S, T, ps, PF, PA, VT, MIX, gat, ngs, cont, onesb, ones32, ident, maskFB, cm32, abias_d, epst, ebias_d):
    op = S.op
    NC = T // 64
    NP = T // 128
    NSB = T // 512
    HB = 2048
    NHB = T // HB
    PFt = lambda kind, h: PF[kind * 1024 + h * 128: kind * 1024 + (h + 1) * 128, :]

    with ExitStack() as es:
      if True:
          E = es.enter_context
          sb = lambda n_, s_, d_: _sb(nc, n_, s_, d_)
          qf = E(sb("h_q", [128, HB], F32))
          zz = [E(sb("h_z%d" % i, [128, HB], F32)) for i in range(2)]
          Lb = E(sb("h_L", [128, HB], F32))
          Pb = E(sb("h_P", [128, HB], F32))
          E1 = E(sb("h_E1", [128, HB], F32))
          vt = E(sb("h_vt", [128, NP, 128], BF16))
          qg = [E(sb("h_qg%d" % i, [128, T], BF16)) for i in range(2)]
          kg = E(sb("h_kg", [128, HB], BF16))
          kgp = E(sb("h_kgp", [128, HB], BF16))
          ATm = [E(sb("h_AT%d" % i, [128, NP, 128], BF16)) for i in range(2)]
          Sbf = [E(sb("h_S%d" % i, [128, NC, 128], BF16)) for i in range(2)]
          xtmp = E(sb("h_xtmp", [128, 4, 128], BF16))
          kgt = [E(sb("h_kt%d" % i, [128, NP, 128], BF16)) for i in range(2)]
          dvec = [E(sb("h_dv%d" % i, [128, NC], F32)) for i in range(2)]
          S32 = [[E(sb("h_S32_%d%d" % (i, j), [128, 128], F32)) for j in range(2)] for i in range(2)]
          gbuf = [E(sb("h_g%d" % i, [128, 512], F32)) for i in range(2)]
          osq2 = [E(sb("h_osq%d" % i, [128, 512], F32)) for i in range(2)]
          rstd2 = [E(sb("h_rstd%d" % i, [128, 512], F32)) for i in range(2)]
          t12 = [E(sb("h_t1%d" % i, [128, 512], F32)) for i in range(2)]
          ob = [E(sb("h_ob%d" % i, [128, 512], BF16)) for i in range(2)]
          maskS = cm32[:, 256:256 + HB]

          for h in range(NHR):
              op("sp", lambda h=h: _dsplit(nc.sync, vt[:], VT[:, h * 128:(h + 1) * 128].rearrange("(n p) f -> p n f", p=128), NP),
                 reads=["VT"], writes=["vt"], dkey="h_vt")
              for hb in range(NHB):
                  c0 = hb * HB
                  op("sp", lambda h=h, c0=c0: nc.sync.dma_start(out=qf[:], in_=PFt(0, h)[:, c0:c0 + HB]), reads=["PF"], writes=["qf"], dkey="h_q")
                  for di in range(2):
                      z = zz[di]
                      lbc = gat[:, 3 * di, h:h + 1]
                      omlc = gat[:, 3 * di + 1, h:h + 1]
                      nomlc = gat[:, 3 * di + 2, h:h + 1]
                      op("sp", lambda h=h, c0=c0, di=di, z=z: nc.sync.dma_start(out=z[:], in_=PFt(1 + di, h)[:, c0:c0 + HB]),
                         reads=["PF"], writes=[("z", di)], dkey=("h_z", di))
                      op("act", lambda z=z: nc.scalar.activation(out=z[:], in_=z[:], func=AF.Sigmoid), reads=[("z", di)], writes=[("z", di)])
                      op("act", lambda z=z, omlc=omlc, lbc=lbc: nc.scalar.activation(out=Lb[:], in_=z[:], func=AF.Ln, scale=omlc, bias=lbc),
                         reads=[("z", di), "gat"], writes=["Lb"])
                      op("dve", lambda z=z, nomlc=nomlc, omlc=omlc: nc.vector.tensor_scalar(out=z[:], in0=z[:], scalar1=nomlc, scalar2=omlc,
                                                                                             op0=ALU.mult, op1=ALU.add),
                         reads=[("z", di), "gat"], writes=[("z", di)])
                      op("dve", lambda: nc.vector.tensor_tensor_scan(out=Pb[:], data0=maskS, data1=Lb[:], initial=0.0, op0=ALU.mult, op1=ALU.add),
                         reads=["Lb", "cm32"], writes=["Pb"])
                      if di == 0:
                          G, Gtok = Pb, "Pb"
                      else:
                          op("dve", lambda: nc.vector.tensor_tensor(out=Lb[:], in0=Lb[:], in1=Pb[:], op=ALU.subtract), reads=["Lb", "Pb"], writes=["Lb"])
                          op("dve", lambda: nc.vector.tensor_tensor(
                              out=Lb[:].rearrange("p (c t) -> p c t", t=64), in0=Lb[:].rearrange("p (c t) -> p c t", t=64),
                              in1=Pb[:].rearrange("p (c t) -> p c t", t=64)[:, :, 63:64].to_broadcast([128, HB // 64, 64]), op=ALU.add),
                             reads=["Lb", "Pb"], writes=["Lb"])
                          G, Gtok = Lb, "Lb"
                      op("act", lambda G=G: nc.scalar.activation(out=E1[:], in_=G[:], func=AF.Exp), reads=[Gtok], writes=["E1"])
                      op("act", lambda G=G: nc.scalar.activation(out=G[:], in_=G[:], func=AF.Exp, scale=-1.0), reads=[Gtok], writes=[Gtok])
                      op("pool", lambda di=di, c0=c0: nc.gpsimd.tensor_tensor(out=qg[di][:, c0:c0 + HB], in0=qf[:], in1=E1[:], op=ALU.mult),
                         reads=["qf", "E1"], writes=[("qg", di)])
                      dpos = 63 if di == 0 else 0
                      op("pool", lambda di=di, c0=c0, dpos=dpos: nc.gpsimd.tensor_copy(
                          out=dvec[di][:, c0 // 64:(c0 + HB) // 64], in_=E1[:].rearrange("p (c t) -> p c t", t=64)[:, :, dpos]),
                         reads=["E1"], writes=[("dvec", di)])
                      op("dve", lambda z=z, G=G: nc.vector.tensor_tensor(out=kg[:], in0=z[:], in1=G[:], op=ALU.mult),
                         reads=[("z", di), Gtok], writes=["kg"])
                      op("pool", lambda di=di, c0=c0: nc.gpsimd.tensor_tensor(
                          out=kgp[:].rearrange("p (c t) -> p c t", t=64), in0=kg[:].rearrange("p (c t) -> p c t", t=64),
                          in1=dvec[di][:, c0 // 64:(c0 + HB) // 64].unsqueeze(2).to_broadcast([128, HB // 64, 64]), op=ALU.mult),
                         reads=["kg", ("dvec", di)], writes=["kgp"])
                      for sbi in range(HB // 512):
                          n0 = (c0 + sbi * 512) // 128
                          def tr(sbi=sbi):
                              last = None
                              for p in range(4):
                                  cs = sbi * 512 + p * 128
                                  last = nc.tensor.matmul(ps[0][:, p * 128:(p + 1) * 128], lhsT=kgp[:, cs:cs + 128], rhs=ident[:], start=True, stop=True)
                              return last
                          op("pe", tr, reads=["kgp", "ident"], writes=[("ps", 0)])
                          op("act", lambda di=di, n0=n0: nc.scalar.copy(out=kgt[di][:, n0:n0 + 4, :].rearrange("p a b -> p (a b)"), in_=ps[0][:]),
                             reads=[("ps", 0)], writes=[("kgt", di)])
                          def atm(sbi=sbi, di=di, c0=c0):
                              last = None
                              for p in range(4):
                                  cs = sbi * 512 + p * 128
                                  last = nc.tensor.matmul(ps[1][:, p * 128:(p + 1) * 128], lhsT=kg[:, cs:cs + 128], rhs=qg[di][:, c0 + cs:c0 + cs + 128],
                                                          start=True, stop=True)
                              return last
                          op("pe", atm, reads=["kg", ("qg", di)], writes=[("ps", 1)])
                          if di == 0:
                              op("dve", lambda n0=n0: nc.vector.tensor_tensor(
                                  out=ATm[0][:, n0:n0 + 4, :], in0=ps[1][:].rearrange("p (a b) -> p a b", a=4),
                                  in1=maskFB[:, 0:1, :].to_broadcast([128, 4, 128]), op=ALU.mult),
                                 reads=[("ps", 1), "maskFB"], writes=[("ATm", 0)])
                          else:
                              op("dve", lambda: nc.vector.tensor_tensor(
                                  out=xtmp[:], in0=ps[1][:].rearrange("p (a b) -> p a b", a=4),
                                  in1=maskFB[:, 1:2, :].to_broadcast([128, 4, 128]), op=ALU.mult),
                                 reads=[("ps", 1), "maskFB"], writes=["xtmp"])
                              op("pool", lambda n0=n0: nc.gpsimd.tensor_tensor(out=ATm[0][:, n0:n0 + 4, :], in0=ATm[0][:, n0:n0 + 4, :], in1=xtmp[:], op=ALU.add),
                                 reads=["xtmp", ("ATm", 0)], writes=[("ATm", 0)])
              orders = [list(range(NC)), list(range(NC - 1, -1, -1))]
              curs = [0, 0]
              for di in range(2):
                  op("dve", lambda di=di: nc.vector.memset(S32[di][0][:], 0.0), writes=[("S32", di, 0)])
                  op("pool", lambda di=di, c=orders[di][0]: nc.gpsimd.memset(Sbf[di][:, c, :], 0.0), writes=[("Sbf", di)])
              kvi = 0
              for i in range(NC - 1):
                  for di in range(2):
                      c = orders[di][i]
                      nxt = orders[di][i + 1]
                      cur = curs[di]
                      kb = (2, 6, 7)[kvi % 3]
                      kvi += 1
                      r0 = (c % 2) * 64
                      op("pe", lambda di=di, c=c, r0=r0, kb=kb: nc.tensor.matmul(
                          ps[kb][:, 0:128], lhsT=kgt[di][r0:r0 + 64, c // 2, :], rhs=vt[r0:r0 + 64, c // 2, :], start=True, stop=True),
                         reads=[("kgt", di), "vt"], writes=[("ps", kb)])
                      op("dve", lambda di=di, c=c, kb=kb, cur=cur: nc.vector.scalar_tensor_tensor(
                          out=S32[di][1 - cur][:], in0=S32[di][cur][:], scalar=dvec[di][:, c:c + 1], in1=ps[kb][:, 0:128],
                          op0=ALU.mult, op1=ALU.add),
                         reads=[("S32", di, cur), ("ps", kb), ("dvec", di)], writes=[("S32", di, 1 - cur)])
                      cur = 1 - cur
                      curs[di] = cur
                      if (di == 0 and nxt == NC // 2) or (di == 1 and nxt == NC // 2 - 1):
                          op("dve", lambda di=di, cur=cur: nc.vector.tensor_scalar(out=S32[di][cur][:], in0=S32[di][cur][:], scalar1=cont[:, 0:1],
                                                                                   scalar2=None, op0=ALU.mult),
                             reads=[("S32", di, cur), "cont"], writes=[("S32", di, cur)])
                      op("act", lambda di=di, nxt=nxt, cur=cur: nc.scalar.copy(out=Sbf[di][:, nxt, :], in_=S32[di][cur][:]),
                         reads=[("S32", di, cur)], writes=[("Sbf", di)])
              def out_stages(j):
                  s = j % 2
                  bo = 3 + s
                  msb = (5, 2)[s]
                  osq_, rstd_, t1_ = osq2[s], rstd2[s], t12[s]
                  def outmm():
                      last = None
                      for p in range(4):
                          n = j * 4 + p
                          for cc in range(2):
                              c = 2 * n + cc
                              ccols = slice(p * 128 + cc * 64, p * 128 + cc * 64 + 64)
                              nc.tensor.matmul(ps[bo][:, ccols], lhsT=vt[:, n, :], rhs=ATm[0][:, n, cc * 64:(cc + 1) * 64], start=True, stop=False)
                              nc.tensor.matmul(ps[bo][:, ccols], lhsT=Sbf[0][:, c, :], rhs=qg[0][:, c * 64:(c + 1) * 64], start=False, stop=False)
                              last = nc.tensor.matmul(ps[bo][:, ccols], lhsT=Sbf[1][:, c, :], rhs=qg[1][:, c * 64:(c + 1) * 64], start=False, stop=True)
                      return last
                  def st0():
                      op("sp", lambda: nc.sync.dma_start(out=gbuf[s][:], in_=PFt(3, h)[:, j * 512:(j + 1) * 512]),
                         reads=["PF"], writes=[("gbuf", s)], dkey=("h_g", s))
                      op("pe", outmm, reads=["vt", ("ATm", 0), ("Sbf", 0), ("Sbf", 1), ("qg", 0), ("qg", 1)], writes=[("ps", bo)])
                  return [
                      st0,
                      lambda: op("act", lambda: nc.scalar.activation(out=osq_[:], in_=ps[bo][:], func=AF.Square), reads=[("ps", bo)], writes=[("osq", s)]),
                      lambda: op("pe", lambda: nc.tensor.matmul(ps[msb][:], lhsT=ones32[:], rhs=osq_[:], start=True, stop=True),
                                 reads=[("osq", s), "ones32"], writes=[("ps", msb)]),
                      lambda: op("act", lambda: nc.scalar.activation(out=rstd_[:], in_=ps[msb][:], func=AF.Sqrt, bias=epst[:, 2:3], scale=1.0),
                                 reads=[("ps", msb), "epst"], writes=[("rstd", s)]),
                      lambda: op("dve", lambda: nc.vector.reciprocal(out=rstd_[:], in_=rstd_[:]), reads=[("rstd", s)], writes=[("rstd", s)]),
                      lambda: op("act", lambda: nc.scalar.activation(out=gbuf[s][:], in_=gbuf[s][:], func=AF.Silu), reads=[("gbuf", s)], writes=[("gbuf", s)]),
                      lambda: op("dve", lambda: nc.vector.tensor_tensor(out=t1_[:], in0=ps[bo][:], in1=rstd_[:], op=ALU.mult),
                                 reads=[("ps", bo), ("rstd", s)], writes=[("t1", s)]),
                      lambda: op("pool", lambda: nc.gpsimd.tensor_tensor(out=t1_[:], in0=t1_[:], in1=gbuf[s][:], op=ALU.mult),
                                 reads=[("t1", s), ("gbuf", s)], writes=[("t1", s)]),
                      lambda: op("act", lambda: nc.scalar.activation(out=ob[s][:], in_=t1_[:], func=AF.Identity, scale=ngs[:, h:h + 1]),
                                 reads=[("t1", s), "ngs"], writes=[("ob", s)]),
                      lambda: op("pool", lambda: nc.gpsimd.dma_start(out=MIX[h * 128:(h + 1) * 128, j * 512:(j + 1) * 512], in_=ob[s][:]),
                                 reads=[("ob", s)], writes=["MIX"], dkey=("h_ob", s)),
                  ]
              for j0 in range(0, NSB, 2):
                  stA = out_stages(j0)
                  stB = out_stages(j0 + 1) if j0 + 1 < NSB else None
                  for k in range(len(stA)):
                      stA[k]()
                      if stB is not None:
                          stB[k]()

    S.barrier()
    with ExitStack() as es:
        E = es.enter_context
        sb = lambda n_, s_, d_: _sb(nc, n_, s_, d_)
        PAD = 1024
        qa = E(sb("a_q", [128, T], BF16))
        kp = E(sb("a_k", [128, PAD + T + PAD], BF16))
        abias = E(sb("a_bias", [128, 24, 256], F32))
        NUM = E(sb("a_num", [128, T], F32))
        DEN = E(sb("a_den", [128, T], F32))
        vs = E(sb("a_vs", [128, 48 * T // 4096 + 16, 128], BF16))
        sc = [E(sb("a_sc%d" % i, [128, 256], F32)) for i in range(2)]
        PT = [E(sb("a_pt%d" % i, [128, 256], BF16)) for i in range(3)]
        attb = E(sb("a_out", [128, T], BF16))
        ebias = E(sb("a_ebias", [128, 24, 256], F32))
        ncont = E(sb("a_ncont", [128, 1], F32))
        PTe = E(sb("a_pte", [128, 128], BF16))
        sce = E(sb("a_sce", [128, 128], F32))
        EE = E(sb("a_ee", [128, 128], BF16))
        PTd = E(sb("a_ptd", [128, 128], BF16))
        op("sp", lambda: nc.sync.dma_start(out=abias[:].rearrange("p a b -> p (a b)"), in_=abias_d), writes=["abias"], dkey="a_bias")
        op("sp", lambda: nc.sync.dma_start(out=ebias[:].rearrange("p a b -> p (a b)"), in_=ebias_d), writes=["ebias"], dkey="a_ebias")
        op("dve", lambda: nc.vector.tensor_scalar(out=ncont[:], in0=cont[:], scalar1=-1.0, scalar2=1.0, op0=ALU.mult, op1=ALU.add),
           reads=["cont"], writes=["ncont"])
        op("pool", lambda: nc.gpsimd.memset(kp[:, 0:PAD], 0.0), writes=[("kp", 0)])
        op("pool", lambda: nc.gpsimd.memset(kp[:, PAD + T:PAD + T + PAD], 0.0), writes=[("kp", 0)])
        scale = float(HD ** -0.5)
        qaB = E(sb("a_qB", [128, T], BF16))
        kpB = E(sb("a_kB", [128, PAD + T + PAD], BF16))
        vsB = E(sb("a_vsB", [128, 48 * T // 4096 + 16, 128], BF16))
        PT.append(E(sb("a_pt3", [128, 256], BF16)))
        sc.append(E(sb("a_sc2", [128, 256], F32)))
        op("pool", lambda: nc.gpsimd.memset(kpB[:, 0:PAD], 0.0), writes=[("kp", 1)])
        op("pool", lambda: nc.gpsimd.memset(kpB[:, PAD + T:PAD + T + PAD], 0.0), writes=[("kp", 1)])
        qa2, kp2, vs2 = [qa, qaB], [kp, kpB], [vs, vsB]
        mb2 = [E(sb("a_mb%d" % i, [128, 3, 256], F32)) for i in range(2)]
        EEn2 = [E(sb("a_een%d" % i, [128, 3, 128], BF16)) for i in range(2)]
        mtmp = E(sb("a_mtmp", [128, 3, 128], F32))
        negn = E(sb("a_negn", [128, 1], F32))
        op("dve", lambda: nc.vector.tensor_scalar(out=negn[:], in0=ncont[:], scalar1=NEG, scalar2=None, op0=ALU.mult), reads=["ncont"], writes=["negn"])
        NPT, NSC = len(PT), len(sc)

        def load_head(h):
            hp = h % 2
            op("sp", lambda: nc.sync.dma_start(out=qa2[hp][:], in_=PA[h * 128:(h + 1) * 128, :]), reads=["PA"], writes=[("qa", hp)], dkey=("a_q", hp))
            op("sp", lambda: nc.sync.dma_start(out=kp2[hp][:, PAD:PAD + T], in_=PA[1024 + h * 128:1024 + (h + 1) * 128, :]),
               reads=["PA"], writes=[("kp", hp)], dkey=("a_k", hp))

        def load_v(h, pi, vp):
            d = PATTERNS[pi]
            L = T // d
            nq = L // 128
            nk = nq + 1
            vsrc = VT[:, 1024 + h * 128:1024 + (h + 1) * 128]
            v4 = vs2[vp][:, 0:d * nk, :].rearrange("p (r k) f -> p r k f", r=d)
            def ldv():
                outs = []
                vr = vsrc.rearrange("(i r) f -> r i f", r=d)
                for r in range(d):
                    outs.append(nc.sync.dma_start(out=v4[64:128, r, 0, :], in_=vr[r, 0:64, :]))
                    outs.append(nc.sync.dma_start(out=v4[0:64, r, nq, :], in_=vr[r, L - 64:L, :]))
                    if nq > 1:
                        outs += _dsplit(nc.sync, v4[:, r, 1:nq, :], vr[r, 64:L - 64, :].rearrange("(k a) f -> a k f", a=128), nq - 1)
                return outs
            op("sp", ldv, reads=["VT"], writes=[("vs", vp)], dkey=("a_vs", vp))

        seq = [(h, pi) for h in range(NHR) for pi in range(3)]
        jobs = []
        kcnt = 0
        for si, (h, pi) in enumerate(seq):
            d = PATTERNS[pi]
            hp = h % 2
            vp = si % 2
            L = T // d
            nq = L // 128
            nk = nq + 1
            qaT, kpT, vsT = qa2[hp], kp2[hp], vs2[vp]
            qn = 0
            pend = []
            first_tile = True
            for r in range(d):
                for kt in range(nk):
                    b0 = 128 if kt == 0 else 0
                    b1 = 128 if kt == nq else 256
                    ks = PAD + (128 * kt - 64) * d + r
                    qs = (128 * (kt - 1) + b0) * d + r
                    nqc = b1 - b0
                    sbk = 6 + kcnt % 2
                    s2 = kcnt % NSC
                    s3 = kcnt % NPT
                    kcnt += 1
                    lhs = kpT[:, _ss(ks, 128, d)]
                    rhs = qaT[:, _ss(qs, nqc, d)]
                    mid = (nq % 2 == 0 and kt == nq // 2)
                    ph = pi * 8 + h
                    if mid:
                        btile, bidx = mb2[hp], pi
                    else:
                        btile, bidx = (ebias if (kt == 0 or kt == nq) else abias), ph
                    pre = None
                    tile_i = r * nk + kt
                    if tile_i == 0 and pi == 0:
                        def pre(h=h, hp=hp):
                            H0, H1 = slice(0, 64), slice(64, 128)
                            hs = _ss(h, 3, 8)
                            mbT, EEnT = mb2[hp], EEn2[hp]
                            tk = ("mb", hp)
                            op("act", lambda: nc.scalar.copy(out=mbT[H0, :, 0:128], in_=abias[H0, hs, 0:128]), reads=["abias"], writes=[tk])
                            op("dve", lambda: nc.vector.tensor_scalar(out=mbT[H1, :, 0:128], in0=abias[H1, hs, 0:128], scalar1=negn[H1, 0:1], scalar2=None,
                                                                      op0=ALU.add), reads=["abias", "negn"], writes=[tk])
                            op("dve", lambda: nc.vector.tensor_scalar(out=mbT[H0, :, 128:256], in0=abias[H0, hs, 128:256], scalar1=negn[H0, 0:1], scalar2=None,
                                                                      op0=ALU.add), reads=["abias", "negn"], writes=[tk])
                            op("dve", lambda: nc.vector.tensor_scalar(out=mtmp[H1, :, :], in0=ebias[H1, hs, 128:256], scalar1=ncont[H1, 0:1], scalar2=None,
                                                                      op0=ALU.mult), reads=["ebias", "ncont"], writes=["mtmp"])
                            op("dve", lambda: nc.vector.scalar_tensor_tensor(out=mbT[H1, :, 128:256], in0=abias[H1, hs, 128:256], scalar=cont[H1, 0:1],
                                                                             in1=mtmp[H1, :, :], op0=ALU.mult, op1=ALU.add),
                               reads=["abias", "cont", "mtmp"], writes=[tk])
                            op("act", lambda: nc.scalar.activation(out=mtmp[H1, :, :], in_=ebias[H1, hs, 0:128], func=AF.Exp), reads=["ebias", "mtmp"], writes=["mtmp"])
                            op("dve", lambda: nc.vector.tensor_scalar(out=EEnT[H1, :, :], in0=mtmp[H1, :, :], scalar1=ncont[H1, 0:1], scalar2=None,
                                                                      op0=ALU.mult), reads=["mtmp", "ncont"], writes=[tk])
                    if tile_i == 3:
                        nxt = seq[si + 1] if si + 1 < len(seq) else None
                        def pre(nxt=nxt, si=si, h=h, pi=pi):
                            if nxt is not None:
                                if nxt[1] == 1 and nxt[0] + 1 < NHR:
                                    load_head(nxt[0] + 1)
                                load_v(nxt[0], nxt[1], (si + 1) % 2)

                    def s1(pre=pre, sbk=sbk, s2=s2, s3=s3, b0=b0, b1=b1, lhs=lhs, rhs=rhs, mid=mid, ph=ph, btile=btile, bidx=bidx, hp=hp, hook=None):
                        if pre is not None:
                            pre()
                        op("pe", lambda: nc.tensor.matmul(ps[sbk][:, b0:b1], lhsT=lhs, rhs=rhs, start=True, stop=True),
                           reads=[("kp", hp), ("qa", hp)], writes=[("ps", sbk)])
                        op("dve", lambda: nc.vector.scalar_tensor_tensor(
                            out=sc[s2][:, b0:b1], in0=ps[sbk][:, b0:b1], scalar=scale, in1=btile[:, bidx, b0:b1], op0=ALU.mult, op1=ALU.add),
                           reads=[("ps", sbk), "abias", "ebias", ("mb", hp)], writes=[("sc", s2)])
                        op("act", lambda: nc.scalar.activation(out=PT[s3][:, b0:b1], in_=sc[s2][:, b0:b1], func=AF.Exp),
                           reads=[("sc", s2)], writes=[("PT", s3)])

                    vidx = r * nk + kt
                    evac = None
                    if kt >= 1:
                        pend.append((r, kt - 1, (qn - 1) % 4))
                        if (qn - 1) % 4 == 3:
                            bsel = ((qn - 1) // 4) % 2
                            groups = []
                            for (rr, kq, sl) in pend:
                                if groups and groups[-1][0] == rr and groups[-1][1] + groups[-1][3] == kq:
                                    groups[-1][3] += 1
                                else:
                                    groups.append([rr, kq, sl, 1])
                            evac = (bsel, groups)
                            pend = []
                    last_of_head = (pi == 2 and r == d - 1 and kt == nk - 1)

                    def s2f(kt=kt, s3=s3, vidx=vidx, qn=qn, nq=nq, mid=mid, vsT=vsT, vp=vp, evac=evac, d=d, pi=pi, last_of_head=last_of_head, h=h, hp=hp):
                        def pv():
                            last = None
                            if kt >= 1:
                                q = qn - 1
                                nb, db, col = 2 + (q // 4) % 2, 4 + (q // 4) % 2, (q % 4) * 128
                                rows = slice(0, 64) if kt == nq else slice(0, 128)
                                nc.tensor.matmul(ps[nb][:, col:col + 128], lhsT=vsT[rows, vidx, :], rhs=PT[s3][rows, 0:128], start=False, stop=True)
                                last = nc.tensor.matmul(ps[db][:, col:col + 128], lhsT=onesb[:, :], rhs=PT[s3][:, 0:128], start=False, stop=(not mid))
                                if mid:
                                    last = nc.tensor.matmul(ps[db][:, col:col + 128], lhsT=onesb[64:128, :], rhs=EEn2[hp][64:128, pi, :], start=False, stop=True)
                            if kt <= nq - 1:
                                q = qn
                                nb, db, col = 2 + (q // 4) % 2, 4 + (q // 4) % 2, (q % 4) * 128
                                rows = slice(64, 128) if kt == 0 else slice(0, 128)
                                nc.tensor.matmul(ps[nb][:, col:col + 128], lhsT=vsT[rows, vidx, :], rhs=PT[s3][rows, 128:256], start=True, stop=False)
                                last = nc.tensor.matmul(ps[db][:, col:col + 128], lhsT=onesb[rows, :], rhs=PT[s3][rows, 128:256], start=True, stop=False)
                            return last
                        wr = set()
                        if kt >= 1:
                            wr.add(((qn - 1) // 4) % 2)
                        if kt <= nq - 1:
                            wr.add((qn // 4) % 2)
                        op("pe", pv, reads=[("PT", s3), ("vs", vp), "onesb"] + ([("mb", hp)] if mid else []),
                           writes=[("ps", 2 + w) for w in wr] + [("ps", 4 + w) for w in wr])
                        if evac is not None:
                            bsel, groups = evac
                            for (rr, kq, sl, n) in groups:
                                a0 = rr + d * 128 * kq
                                for (bank, dst, tokn) in ((2 + bsel, NUM, "NUM"), (4 + bsel, DEN, "DEN")):
                                    dv = dst[:, _ss(a0, 128 * n, d)]
                                    src_ = ps[bank][:, sl * 128:(sl + n) * 128]
                                    if pi == 0:
                                        op("act", lambda dv=dv, src_=src_: nc.scalar.copy(out=dv, in_=src_), reads=[("ps", bank)], writes=[tokn])
                                    else:
                                        op("dve", lambda dv=dv, src_=src_: nc.vector.tensor_tensor(out=dv, in0=dv, in1=src_, op=ALU.add),
                                           reads=[("ps", bank), tokn], writes=[tokn])
                        if last_of_head:
                            op("act", lambda: nc.scalar.activation(out=DEN[:], in_=DEN[:], func=AF.Ln), reads=["DEN"], writes=["DEN"])
                            op("act", lambda: nc.scalar.activation(out=DEN[:], in_=DEN[:], func=AF.Exp, scale=-1.0), reads=["DEN"], writes=["DEN"])
                            op("pool", lambda: nc.gpsimd.tensor_tensor(out=attb[:], in0=NUM[:], in1=DEN[:], op=ALU.mult), reads=["NUM", "DEN"], writes=["attb"])
                            op("pool", lambda: nc.gpsimd.dma_start(out=MIX[1024 + h * 128:1024 + (h + 1) * 128, :], in_=attb[:]),
                               reads=["attb"], writes=["MIX"], dkey="a_out")
                    jobs.append((s1, s2f))
                    if kt <= nq - 1:
                        qn += 1
        load_head(0)
        load_v(0, 0, 0)
        LOOK = 2
        for i in range(len(jobs) + LOOK):
            if i < len(jobs):
                jobs[i][0]()
            if i >= LOOK:
                jobs[i - LOOK][1]()


def _consts():
    slopes = 2.0 ** (-8.0 * (np.arange(NH, dtype=np.float64) + 1.0) / NH)
    a = np.arange(128)[:, None]
    b = np.arange(256)[None, :]
    rel = a - b + 64
    ab = np.zeros((128, 24, 256), np.float32)
    for pi, d in enumerate(PATTERNS):
        for h in range(NH):
            v = -slopes[h] * d * np.abs(rel)
            v = np.where(np.abs(rel) <= 64, v, NEG)
            ab[:, pi * 8 + h, :] = v.astype(np.float32)
    eb = ab.copy()
    m_edge = ((a >= 64) & (b >= 128) & (b < 192) & (rel < 0)) | ((a >= 64) & (b == a))
    eb[np.broadcast_to(m_edge[:, None, :], eb.shape)] = NEG
    s = np.arange(128)[:, None]
    t = np.arange(128)[None, :]
    same = (s // 64) == (t // 64)
    mF = (same & (s <= t)).astype(np.float32)
    mB = (same & (s >= t)).astype(np.float32)
    scan = np.ones((128, 2048), np.float32)
    scan[:, ::64] = 0.0
    cmask = np.concatenate([mF, mB, scan], axis=1)
    return ab.reshape(128, 24 * 256), np.ascontiguousarray(cmask), np.ascontiguousarray(eb.reshape(128, 24 * 256))


def _pc(v):
    return np.ascontiguousarray(np.asarray(v, np.float32).reshape(-1, 128).T)


_NC_CACHE = {}


def run_cores(slots, conts, W, T, debug=False):
    key = (T, debug)
    if key not in _NC_CACHE:
        _NC_CACHE[key] = build_program(T, debug)
    nc = _NC_CACHE[key]
    ab, cmask, eb = _consts()
    lnp = np.concatenate([_pc(W[k][0]) for k in ("ln1_g", "ln1_b", "ln2_g", "ln2_b", "ln3_g", "ln3_b")], axis=1)
    hgp = np.concatenate([_pc(W["hgrn_lb_fwd"][0]), _pc(W["hgrn_lb_fwd"][1]), _pc(W["hgrn_lb_bwd"][0]), _pc(W["hgrn_lb_bwd"][1]),
                          _pc(W["hgrn_norm_g"][0])], axis=1)
    common = {
        "wg1": np.ascontiguousarray(W["ffn1_w_gate"][0]), "wu1": np.ascontiguousarray(W["ffn1_w_up"][0]),
        "wd1": np.ascontiguousarray(W["ffn1_w_down"][0]),
        "wg2": np.ascontiguousarray(W["ffn2_w_gate"][0]), "wu2": np.ascontiguousarray(W["ffn2_w_up"][0]),
        "wd2": np.ascontiguousarray(W["ffn2_w_down"][0]),
        "w_in": np.ascontiguousarray(W["w_in"][0]), "w_out": np.ascontiguousarray(W["w_out"][0]),
        "lnp": np.ascontiguousarray(lnp), "hgp": np.ascontiguousarray(hgp), "abias": ab, "cmask": cmask, "ebias": eb,
    }
    in_maps = []
    for x, c in zip(slots, conts):
        m = dict(common)
        m["xT"] = np.ascontiguousarray(np.asarray(x, np.float32).T)
        m["cont"] = np.full((128, 1), c, np.float32)
        in_maps.append(m)
    res = run_bass_kernel_spmd(nc, in_maps, core_ids=list(range(len(slots))))
    return res.results


def kernel(**inputs):
    W = {k: np.asarray(v) for k, v in inputs.items()}
    xp = W["x_prompt"]
    xs = W["x_sample"]
    T = 4096
    slots = [xp[b] for b in range(4)]
    pairs = [np.concatenate([xs[0], xs[1]], axis=0), np.concatenate([xs[2], xs[3]], axis=0)]
    slots += pairs + pairs
    conts = [1.0] * 4 + [0.0] * 4
    res = run_cores(slots, conts, W, T)
    yp = np.stack([np.ascontiguousarray(res[b]["yT"].T) for b in range(4)], axis=0).astype(np.float32)
    ys = []
    for i in range(2):
        y = np.ascontiguousarray(res[4 + i]["yT"].T)
        ys.append(y[:2048])
        ys.append(y[2048:])
    ys = np.stack(ys, axis=0).astype(np.float32)
    return (yp, ys)
```

```python
import numpy as np
from contextlib import ExitStack
import concourse.bass as bass
import concourse.mybir as mybir
from concourse.bass_utils import run_bass_kernel_spmd

F32 = mybir.dt.float32
BF16 = mybir.dt.bfloat16
AF = mybir.ActivationFunctionType
ALU = mybir.AluOpType

D = 2048
DFF = 5632
NKC = D // 128
NFC = DFF // 128
HD = 128
NH = 8
NHR = NH
ALPHA = 2.0 ** 0.25
LN_EPS = 1e-5
RMS_EPS = 1e-6
PATTERNS = (1, 4, 16)
TT = 512
NEG = -30000.0


_UID = [0]


def _sb(nc, name, shape, dtype):
    _UID[0] += 1
    return nc.sbuf_tensor("sb%d_%s" % (_UID[0], name), shape, dtype)


def _ss(start, n, step):
    return slice(start, start + (n - 1) * step + 1, step) if step > 1 else slice(start, start + n)


def _dsplit(eng, out3, in3, n, step=8):
    res = []
    for a in range(0, n, step):
        b = min(n, a + step)
        res.append(eng.dma_start(out=out3[:, a:b, :], in_=in3[:, a:b, :]))
    return res


class _Op:
    __slots__ = ("eng", "sig", "dma")


class Sched:
    def __init__(self, nc, es):
        self.nc = nc
        self.es = es
        self.engs = {"pe": nc.tensor, "act": nc.scalar, "dve": nc.vector, "pool": nc.gpsimd, "sp": nc.sync}
        self.esem = {k: es.enter_context(nc.semaphore("s_" + k)) for k in self.engs}
        self.ecnt = {k: 0 for k in self.engs}
        self.dsem = {}
        self.dcnt = {}
        self.waited = {k: {} for k in self.engs}
        self.last_w = {}
        self.readers = {}
        self.nops = 0

    def op(self, eng, fn, reads=(), writes=(), dkey=None):
        E = self.engs[eng]
        deps = []
        for r in reads:
            w = self.last_w.get(r)
            if w is not None:
                deps.append(w)
        for r in writes:
            w = self.last_w.get(r)
            if w is not None:
                deps.append(w)
            deps.extend(self.readers.get(r, ()))
        wt = self.waited[eng]
        for d in deps:
            if d.eng == eng and eng == "pe" and not d.dma:
                continue
            sem, val = d.sig
            key = id(sem)
            if wt.get(key, 0) < val:
                E.wait_ge(sem, val)
                wt[key] = val
        insts = fn()
        o = _Op()
        o.eng = eng
        if dkey is None:
            inst = insts[-1] if isinstance(insts, (list, tuple)) else insts
            sem = self.esem[eng]
            self.ecnt[eng] += 1
            inst.then_inc(sem, 1)
            o.sig = (sem, self.ecnt[eng])
            o.dma = False
        else:
            dkey = (dkey, eng)
            if dkey not in self.dsem:
                self.dsem[dkey] = self.es.enter_context(self.nc.semaphore("d%d" % len(self.dsem)))
                self.dcnt[dkey] = 0
            sem = self.dsem[dkey]
            if not isinstance(insts, (list, tuple)):
                insts = [insts]
            for inst in insts:
                inst.then_inc(sem, 16)
                self.dcnt[dkey] += 16
            o.sig = (sem, self.dcnt[dkey])
            o.dma = True
        for r in reads:
            self.readers.setdefault(r, []).append(o)
        for r in writes:
            self.last_w[r] = o
            self.readers[r] = []
        self.nops += 1
        return o

    def barrier(self):
        for k, E in self.engs.items():
            wt = self.waited[k]
            for k2, sem in self.esem.items():
                if k2 != k and self.ecnt[k2] > 0 and wt.get(id(sem), 0) < self.ecnt[k2]:
                    E.wait_ge(sem, self.ecnt[k2])
                    wt[id(sem)] = self.ecnt[k2]
            for dk, sem in self.dsem.items():
                if self.dcnt[dk] > 0 and wt.get(id(sem), 0) < self.dcnt[dk]:
                    E.wait_ge(sem, self.dcnt[dk])
                    wt[id(sem)] = self.dcnt[dk]

    def finish(self):
        sp = self.nc.sync
        for k, sem in self.dsem.items():
            if self.dcnt[k] > 0:
                sp.wait_ge(sem, self.dcnt[k])
        for k, sem in self.esem.items():
            if k != "sp" and self.ecnt[k] > 0:
                sp.wait_ge(sem, self.ecnt[k])


def build_program(T, debug=False):
    assert T % 2048 == 0
    NT = T // TT
    NC = T // 64
    nc = bass.Bass("TRN2", target_bir_lowering=False)
    dt = nc.dram_tensor
    kin = "ExternalInput"
    xT = dt("xT", [D, T], F32, kind=kin).ap()
    w_src = {
        "g1": dt("wg1", [D, DFF], F32, kind=kin).ap(), "u1": dt("wu1", [D, DFF], F32, kind=kin).ap(),
        "d1": dt("wd1", [DFF, D], F32, kind=kin).ap(),
        "g2": dt("wg2", [D, DFF], F32, kind=kin).ap(), "u2": dt("wu2", [D, DFF], F32, kind=kin).ap(),
        "d2": dt("wd2", [DFF, D], F32, kind=kin).ap(),
        "in": dt("w_in", [D, 8192], F32, kind=kin).ap(), "out": dt("w_out", [D, D], F32, kind=kin).ap(),
    }
    lnp_d = dt("lnp", [128, 6 * NKC], F32, kind=kin).ap()
    hgp_d = dt("hgp", [128, 5 * NH], F32, kind=kin).ap()
    cont_d = dt("cont", [128, 1], F32, kind=kin).ap()
    abias_d = dt("abias", [128, 24 * 256], F32, kind=kin).ap()
    ebias_d = dt("ebias", [128, 24 * 256], F32, kind=kin).ap()
    cmask_d = dt("cmask", [128, 2 * 128 + 2048], F32, kind=kin).ap()
    yT = dt("yT", [D, T], F32, kind="ExternalOutput").ap()
    okind = "ExternalOutput" if debug else "Internal"
    X1T = dt("X1T", [D, T], F32, kind=okind).ap()
    PF = dt("PF", [4096, T], F32, kind=okind).ap()
    PA = dt("PA", [2048, T], BF16, kind="Internal").ap()
    VT = dt("VT", [T, 2048], BF16, kind="Internal").ap()
    MIX = dt("MIX", [2048, T], BF16, kind=okind).ap()
    WS = {
        "gu1": dt("ws_gu1", [22, 128, 16 * 512], BF16).ap(), "gu2": dt("ws_gu2", [22, 128, 16 * 512], BF16).ap(),
        "d1": dt("ws_d1", [16, 128, NFC * 128], BF16).ap(), "d2": dt("ws_d2", [16, 128, NFC * 128], BF16).ap(),
        "in": dt("ws_in", [16, 128, 16 * 512], BF16).ap(), "out": dt("ws_out", [4, 128, 16 * 512], BF16).ap(),
    }

    with ExitStack() as es:
        E = es.enter_context
        S = Sched(nc, es)
        op = S.op
        ps = [E(nc.psum_tensor("ps%d" % i, [128, 512], F32)) for i in range(8)]

        lnp = E(_sb(nc, "lnp", [128, 6, NKC], F32))
        hgp = E(_sb(nc, "hgp", [128, 5, NH], F32))
        cont = E(_sb(nc, "cont", [128, 1], F32))
        gat = E(_sb(nc, "gat", [128, 6, NH], F32))
        ngs = E(_sb(nc, "ngs", [128, NH], F32))
        ones32 = E(_sb(nc, "ones32", [128, 128], F32))
        onesb = E(_sb(nc, "onesb", [128, 128], BF16))
        ident = E(_sb(nc, "ident", [128, 128], BF16))
        cm32 = E(_sb(nc, "cm32", [128, 2 * 128 + 2048], F32))
        maskFB = E(_sb(nc, "maskFB", [128, 2, 128], F32))

        op("sp", lambda: nc.sync.dma_start(out=lnp[:].rearrange("p a c -> p (a c)"), in_=lnp_d), writes=["lnp"], dkey="c0")
        op("sp", lambda: nc.sync.dma_start(out=hgp[:].rearrange("p a c -> p (a c)"), in_=hgp_d), writes=["hgp"], dkey="c1")
        op("sp", lambda: nc.sync.dma_start(out=cont[:], in_=cont_d), writes=["cont"], dkey="c2")
        op("sp", lambda: nc.sync.dma_start(out=cm32[:], in_=cmask_d), writes=["cm32"], dkey="c3")
        epst = E(_sb(nc, "epst", [128, 3], F32))
        op("dve", lambda: nc.vector.memset(epst[:, 0:1], LN_EPS), writes=["epst"])
        op("dve", lambda: nc.vector.memset(epst[:, 1:2], 4.0 * LN_EPS), writes=["epst"])
        op("dve", lambda: nc.vector.memset(epst[:, 2:3], 128.0 * RMS_EPS), writes=["epst"])
        op("dve", lambda: nc.vector.memset(ones32[:], 1.0), writes=["ones32"])
        op("dve", lambda: nc.vector.memset(onesb[:], 1.0), writes=["onesb"])
        op("dve", lambda: nc.vector.tensor_copy(out=maskFB[:].rearrange("p a c -> p (a c)"), in_=cm32[:, 0:256]),
           reads=["cm32"], writes=["maskFB"])
        tmpi = E(_sb(nc, "tmpi", [128, 128], F32))
        op("dve", lambda: nc.vector.tensor_tensor(out=tmpi[:], in0=cm32[:, 0:128], in1=cm32[:, 128:256], op=ALU.mult),
           reads=["cm32"], writes=["tmpi"])
        op("dve", lambda: nc.vector.tensor_copy(out=ident[:], in_=tmpi[:]), reads=["tmpi"], writes=["ident"])
        for di in range(2):
            op("dve", lambda di=di: nc.vector.tensor_tensor(out=gat[:, 3 * di, :], in0=hgp[:, 2 * di, :], in1=hgp[:, 2 * di + 1, :],
                                                          op=ALU.subtract), reads=["hgp"], writes=["gat"])
            op("act", lambda di=di: nc.scalar.activation(out=gat[:, 3 * di, :], in_=gat[:, 3 * di, :], func=AF.Sigmoid),
               reads=["gat"], writes=["gat"])
            op("dve", lambda di=di: nc.vector.tensor_scalar(out=gat[:, 3 * di + 1, :], in0=gat[:, 3 * di, :], scalar1=-1.0, scalar2=1.0,
                                                          op0=ALU.mult, op1=ALU.add), reads=["gat"], writes=["gat"])
            op("dve", lambda di=di: nc.vector.tensor_scalar(out=gat[:, 3 * di + 2, :], in0=gat[:, 3 * di + 1, :], scalar1=-1.0, scalar2=None,
                                                          op0=ALU.mult), reads=["gat"], writes=["gat"])
        op("dve", lambda: nc.vector.tensor_scalar(out=ngs[:], in0=hgp[:, 4, :], scalar1=float(128.0 ** 0.5), scalar2=None, op0=ALU.mult),
           reads=["hgp"], writes=["ngs"])

        def row_phase(which):
          with ExitStack() as es2:
            E2 = es2.enter_context
            R = E2(_sb(nc, "R", [128, NKC, TT], F32))
            xbf = E2(_sb(nc, "xbf", [128, NKC, TT], BF16))
            aT = E2(_sb(nc, "aT", [128, NFC, TT], BF16))
            wring = [E2(_sb(nc, "wr%d" % i, [128, 16 * 512], BF16)) for i in range(3)]
            wdring = [E2(_sb(nc, "wd%d" % i, [128, NFC * 128], BF16)) for i in range(2)]
            sil = [E2(_sb(nc, "sil%d" % i, [128, TT], F32)) for i in range(2)]
            sq = [E2(_sb(nc, "sq%d" % i, [128, TT], F32)) for i in range(2)]
            acc1 = E2(_sb(nc, "acc1", [128, TT], F32))
            acc2 = E2(_sb(nc, "acc2", [128, TT], F32))
            mean = E2(_sb(nc, "mean", [128, TT], F32))
            m2 = E2(_sb(nc, "m2", [128, TT], F32))
            lnA = E2(_sb(nc, "lnA", [128, TT], F32))
            lnB = E2(_sb(nc, "lnB", [128, TT], F32))
            stg32 = [E2(_sb(nc, "stg32_%d" % i, [128, TT], F32)) for i in range(2)]
            stg16 = [E2(_sb(nc, "stg16_%d" % i, [128, TT], BF16)) for i in range(2)]
            cnt = {"wr": 0, "wd": 0, "stg32": 0, "stg16": 0, "ev": 0}

            def wblock(kind, first, j, srcs):
                slot = cnt["wr"] % 3
                cnt["wr"] += 1
                buf = wring[slot]
                b3 = buf[:].rearrange("p (k n) -> p k n", k=16)
                if first:
                    def ld():
                        r_ = []
                        for (c0, n, src) in srcs:
                            r_ += _dsplit(nc.gpsimd, b3[:, :, c0:c0 + n], src, 16)
                        return r_
                    op("pool", ld, writes=[("wr", slot)], dkey=("wr", slot))
                    op("sp", lambda: nc.sync.dma_start(out=WS[kind][j], in_=buf[:]), reads=[("wr", slot)],
                       writes=[("ws", kind, j)], dkey=("wrs", slot))
                else:
                    op("sp", lambda: nc.sync.dma_start(out=buf[:], in_=WS[kind][j]), reads=[("ws", kind, j)],
                       writes=[("wr", slot)], dkey=("wr", slot))
                return b3, ("wr", slot)

            def wdblock(kind, first, dc, src):
                slot = cnt["wd"] % 2
                cnt["wd"] += 1
                buf = wdring[slot]
                b3 = buf[:].rearrange("p (k n) -> p k n", k=NFC)
                if first:
                    op("pool", lambda: _dsplit(nc.gpsimd, b3, src, NFC), writes=[("wd", slot)], dkey=("wd", slot))
                    op("sp", lambda: nc.sync.dma_start(out=WS[kind][dc], in_=buf[:]), reads=[("wd", slot)],
                       writes=[("ws", kind, dc)], dkey=("wds", slot))
                else:
                    op("sp", lambda: nc.sync.dma_start(out=buf[:], in_=WS[kind][dc]), reads=[("ws", kind, dc)],
                       writes=[("wd", slot)], dkey=("wd", slot))
                return b3, ("wd", slot)

            def mm_group(out_ap, pairs):
                def f():
                    last = None
                    n = len(pairs)
                    for i, (l, r) in enumerate(pairs):
                        last = nc.tensor.matmul(out_ap, lhsT=l, rhs=r, start=(i == 0), stop=(i == n - 1))
                    return last
                return f

            def mm_group_split(out_ap, pairs, wtok, ktoks, btok):
                n = len(pairs)
                for i, (l, r) in enumerate(pairs):
                    op("pe", lambda l=l, r=r, i=i: nc.tensor.matmul(out_ap, lhsT=l, rhs=r, start=(i == 0), stop=(i == n - 1)),
                       reads=[wtok, ktoks[i]], writes=[btok])

            def r_load(dram3, t0, extra_reads, key):
                for g in range(4):
                    ks = slice(4 * g, 4 * g + 4)
                    op("pool", lambda ks=ks: nc.gpsimd.dma_start(out=R[:, ks, :], in_=dram3[:, ks, t0:t0 + TT]),
                       reads=extra_reads, writes=[("R", k) for k in range(4 * g, 4 * g + 4)], dkey=(key, g))

            def r_store(dram3, t0, wtok, key):
                for g in range(4):
                    ks = slice(4 * g, 4 * g + 4)
                    op("pool", lambda ks=ks: nc.gpsimd.dma_start(out=dram3[:, ks, t0:t0 + TT], in_=R[:, ks, :]),
                       reads=[("R", k) for k in range(4 * g, 4 * g + 4)], writes=[wtok], dkey=(key, g))

            def preconvert(items):
                for (kind, j) in items:
                    if kind == "gu2":
                        dst = WS[kind][j].rearrange("p (k n) -> p k n", k=16)
                        g3 = w_src["g2"].rearrange("(k p) n -> p k n", p=128)[:, :, j * 256:(j + 1) * 256]
                        u3 = w_src["u2"].rearrange("(k p) n -> p k n", p=128)[:, :, j * 256:(j + 1) * 256]
                        f = lambda dst=dst, g3=g3, u3=u3: _dsplit(nc.gpsimd, dst[:, :, 0:256], g3, 16) + _dsplit(nc.gpsimd, dst[:, :, 256:512], u3, 16)
                    elif kind == "d2":
                        dst = WS[kind][j].rearrange("p (k n) -> p k n", k=NFC)
                        s3 = w_src["d2"].rearrange("(k p) n -> p k n", p=128)[:, :, j * 128:(j + 1) * 128]
                        f = lambda dst=dst, s3=s3: _dsplit(nc.gpsimd, dst, s3, NFC)
                    else:
                        dst = WS[kind][j].rearrange("p (k n) -> p k n", k=16)
                        s3 = w_src["out"].rearrange("(k p) n -> p k n", p=128)[:, :, j * 512:(j + 1) * 512]
                        f = lambda dst=dst, s3=s3: _dsplit(nc.gpsimd, dst, s3, 16)
                    op("pool", f, writes=[("ws", kind, j)], dkey="cv")

            def epilogue(dc, by, cmul):
                op("dve", lambda: nc.vector.scalar_tensor_tensor(out=R[:, dc, :], in0=R[:, dc, :], scalar=float(cmul), in1=ps[by][:],
                                                                 op0=ALU.mult, op1=ALU.add),
                   reads=[("ps", by), ("R", dc)], writes=[("R", dc)])
                if dc == 0:
                    op("dve", lambda: nc.vector.tensor_copy(out=acc1[:], in_=R[:, 0, :]), reads=[("R", 0)], writes=["acc1"])
                    op("act", lambda: nc.scalar.activation(out=acc2[:], in_=R[:, 0, :], func=AF.Square), reads=[("R", 0)], writes=["acc2"])
                else:
                    s = dc % 2
                    op("dve", lambda: nc.vector.tensor_tensor(out=acc1[:], in0=acc1[:], in1=R[:, dc, :], op=ALU.add),
                       reads=[("R", dc), "acc1"], writes=["acc1"])
                    op("act", lambda: nc.scalar.activation(out=sq[s][:], in_=R[:, dc, :], func=AF.Square), reads=[("R", dc)], writes=[("sq", s)])
                    op("dve", lambda: nc.vector.tensor_tensor(out=acc2[:], in0=acc2[:], in1=sq[s][:], op=ALU.add),
                       reads=[("sq", s), "acc2"], writes=["acc2"])

            def layer_norm(li, eps, final=None):
                op("pe", lambda: nc.tensor.matmul(ps[6][:], lhsT=ones32[:], rhs=acc1[:], start=True, stop=True),
                   reads=["ones32", "acc1"], writes=[("ps", 6)])
                op("pe", lambda: nc.tensor.matmul(ps[7][:], lhsT=ones32[:], rhs=acc2[:], start=True, stop=True),
                   reads=["ones32", "acc2"], writes=[("ps", 7)])
                op("dve", lambda: nc.vector.tensor_scalar(out=mean[:], in0=ps[6][:], scalar1=1.0 / D, scalar2=None, op0=ALU.mult),
                   reads=[("ps", 6)], writes=["mean"])
                op("dve", lambda: nc.vector.tensor_tensor(out=m2[:], in0=mean[:], in1=mean[:], op=ALU.mult), reads=["mean"], writes=["m2"])
                op("dve", lambda: nc.vector.scalar_tensor_tensor(out=m2[:], in0=ps[7][:], scalar=1.0 / D, in1=m2[:], op0=ALU.mult,
                                                                 op1=ALU.subtract), reads=[("ps", 7), "m2"], writes=["m2"])
                ei = {LN_EPS: 0, 4.0 * LN_EPS: 1}[eps]
                op("act", lambda: nc.scalar.activation(out=m2[:], in_=m2[:], func=AF.Sqrt, bias=epst[:, ei:ei + 1], scale=1.0),
                   reads=["m2", "epst"], writes=["m2"])
                op("dve", lambda: nc.vector.reciprocal(out=lnA[:], in_=m2[:]), reads=["m2"], writes=["lnA"])
                for dc in range(NKC):
                    s = dc % 2
                    op("dve", lambda dc=dc, s=s: nc.vector.tensor_tensor(out=sil[s][:], in0=R[:, dc, :], in1=mean[:], op=ALU.subtract),
                       reads=[("R", dc), "mean"], writes=[("sil", s)])
                    op("dve", lambda dc=dc, s=s: nc.vector.tensor_tensor(out=sil[s][:], in0=sil[s][:], in1=lnA[:], op=ALU.mult),
                       reads=[("sil", s), "lnA"], writes=[("sil", s)])
                    if final is not None:
                        t0f, nxt_fn = final
                        so = cnt["stg32"] % 2
                        cnt["stg32"] += 1
                        op("act", lambda dc=dc, s=s, so=so: nc.scalar.activation(out=stg32[so][:], in_=sil[s][:], func=AF.Identity,
                                                                               scale=lnp[:, 2 * li, dc:dc + 1], bias=lnp[:, 2 * li + 1, dc:dc + 1]),
                           reads=[("sil", s), "lnp"], writes=[("stg32", so)])
                        op("pool", lambda dc=dc, so=so: nc.gpsimd.dma_start(out=yT[dc * 128:(dc + 1) * 128, t0f:t0f + TT], in_=stg32[so][:]),
                           reads=[("stg32", so)], writes=[("yT", dc)], dkey=("st32", so))
                        if nxt_fn is not None:
                            nxt_fn(dc)
                        continue
                    op("act", lambda dc=dc, s=s: nc.scalar.activation(out=R[:, dc, :], in_=sil[s][:], func=AF.Identity,
                                                                      scale=lnp[:, 2 * li, dc:dc + 1], bias=lnp[:, 2 * li + 1, dc:dc + 1]),
                       reads=[("sil", s), "lnp"], writes=[("R", dc)])
                    if li != 2:
                        op("act", lambda dc=dc, s=s: nc.scalar.activation(out=xbf[:, dc, :], in_=sil[s][:], func=AF.Identity,
                                                                          scale=lnp[:, 2 * li, dc:dc + 1], bias=lnp[:, 2 * li + 1, dc:dc + 1]),
                           reads=[("sil", s), "lnp"], writes=[("xbf", dc)])

            def ffn(fid, first):
                wg, wu, wd = w_src["g" + fid], w_src["u" + fid], w_src["d" + fid]
                wg3 = wg.rearrange("(k p) n -> p k n", p=128)
                wu3 = wu.rearrange("(k p) n -> p k n", p=128)
                wd3 = wd.rearrange("(k p) n -> p k n", p=128)
                xall = [("xbf", k) for k in range(NKC)]
                for j in range(22):
                    b3, tok = wblock("gu" + fid, first, j, [(0, 256, wg3[:, :, j * 256:(j + 1) * 256]), (256, 256, wu3[:, :, j * 256:(j + 1) * 256])])
                    for c in range(2):
                        fc = 2 * j + c
                        bg, bu, s = (0, 1, 4)[fc % 3], (2, 3, 5)[fc % 3], fc % 2
                        if fc == 0:
                            mm_group_split(ps[bg][:], [(b3[:, k, c * 128:(c + 1) * 128], xbf[:, k, :]) for k in range(NKC)], tok, xall, ("ps", bg))
                        else:
                            op("pe", mm_group(ps[bg][:], [(b3[:, k, c * 128:(c + 1) * 128], xbf[:, k, :]) for k in range(NKC)]),
                               reads=[tok] + xall, writes=[("ps", bg)])
                        op("pe", mm_group(ps[bu][:], [(b3[:, k, 256 + c * 128:256 + (c + 1) * 128], xbf[:, k, :]) for k in range(NKC)]),
                           reads=[tok] + xall, writes=[("ps", bu)])
                        op("act", lambda bg=bg, s=s: nc.scalar.activation(out=sil[s][:], in_=ps[bg][:], func=AF.Silu),
                           reads=[("ps", bg)], writes=[("sil", s)])
                        op("dve", lambda bu=bu, s=s, fc=fc: nc.vector.tensor_tensor(out=aT[:, fc, :], in0=sil[s][:], in1=ps[bu][:], op=ALU.mult),
                           reads=[("sil", s), ("ps", bu)], writes=[("a", fc)])
                aall = [("a", f) for f in range(NFC)]
                for dc in range(NKC):
                    b3, tok = wdblock("d" + fid, first, dc, wd3[:, :, dc * 128:(dc + 1) * 128])
                    by = (4, 5, 0, 1, 2, 3)[dc % 6]
                    if dc == 0:
                        mm_group_split(ps[by][:], [(b3[:, f, :], aT[:, f, :]) for f in range(NFC)], tok, aall, ("ps", by))
                    else:
                        op("pe", mm_group(ps[by][:], [(b3[:, f, :], aT[:, f, :]) for f in range(NFC)]), reads=[tok] + aall, writes=[("ps", by)])
                    epilogue(dc, by, 2.0 * ALPHA)

            def evac(by, dst, use_act):
                if use_act:
                    return op("act", lambda: nc.scalar.copy(out=dst, in_=ps[by][:]), reads=[("ps", by)], writes=[dst_tok[0]])
                return op("dve", lambda: nc.vector.tensor_copy(out=dst, in_=ps[by][:]), reads=[("ps", by)], writes=[dst_tok[0]])

            dst_tok = [None]
            win3 = w_src["in"].rearrange("(k p) n -> p k n", p=128)
            wout3 = w_src["out"].rearrange("(k p) n -> p k n", p=128)

            conv_items = [("gu2", j) for j in range(22)] + [("d2", j) for j in range(16)] + [("out", j) for j in range(4)]
            for t in (range(NT) if which == "A" else []):
                first = (t == 0)
                t0 = t * TT
                Rall = [("R", k) for k in range(NKC)]
                xT3 = xT.rearrange("(k p) n -> p k n", p=128)
                if t == 0:
                    r_load(xT3, 0, [], "ldx")
                else:
                    nper = (len(conv_items) + NT - 2) // (NT - 1)
                    preconvert(conv_items[(t - 1) * nper:t * nper])
                for k in range(NKC):
                    if k % 2:
                        op("act", lambda k=k: nc.scalar.copy(out=xbf[:, k, :], in_=R[:, k, :]), reads=[("R", k)], writes=[("xbf", k)])
                    else:
                        op("dve", lambda k=k: nc.vector.tensor_copy(out=xbf[:, k, :], in_=R[:, k, :]), reads=[("R", k)], writes=[("xbf", k)])
                ffn("1", first)
                layer_norm(0, 4.0 * LN_EPS)
                r_store(X1T.rearrange("(k p) n -> p k n", p=128), t0, ("X1T", t), "stx1")
                if t + 1 < NT:
                    r_load(xT3, t0 + TT, [], "ldx")
                xall = [("xbf", k) for k in range(NKC)]
                for j in range(16):
                    b3, tok = wblock("in", first, j, [(0, 512, win3[:, :, j * 512:(j + 1) * 512])])
                    tokmajor = j in (6, 7, 14, 15)
                    for c in range(4):
                        by = (4, 5, 0, 1, 2, 3)[cnt["ev"] % 6]
                        cnt["ev"] += 1
                        if not tokmajor:
                            if j == 0 and c == 0:
                                mm_group_split(ps[by][:], [(b3[:, k, c * 128:(c + 1) * 128], xbf[:, k, :]) for k in range(NKC)], tok, xall, ("ps", by))
                            else:
                                op("pe", mm_group(ps[by][:], [(b3[:, k, c * 128:(c + 1) * 128], xbf[:, k, :]) for k in range(NKC)]),
                                   reads=[tok] + xall, writes=[("ps", by)])
                            f0 = j * 512 + c * 128
                            if j < 6 or j in (8, 9):
                                row = f0 if j < 6 else 3072 + (f0 - 4096)
                                s = cnt["stg32"] % 2
                                cnt["stg32"] += 1
                                dst_tok[0] = ("stg32", s)
                                evac(by, stg32[s][:], c % 2 == 0)
                                op("pool", lambda s=s, row=row: nc.gpsimd.dma_start(out=PF[row:row + 128, t0:t0 + TT], in_=stg32[s][:]),
                                   reads=[("stg32", s)], writes=[("PF", row // 128, t)], dkey=("st32", s))
                            else:
                                row = f0 - 5120
                                s = cnt["stg16"] % 2
                                cnt["stg16"] += 1
                                dst_tok[0] = ("stg16", s)
                                evac(by, stg16[s][:], c % 2 == 0)
                                op("pool", lambda s=s, row=row: nc.gpsimd.dma_start(out=PA[row:row + 128, t0:t0 + TT], in_=stg16[s][:]),
                                   reads=[("stg16", s)], writes=[("PA", row // 128, t)], dkey=("st16", s))
                        else:
                            op("pe", mm_group(ps[by][:], [(xbf[:, k, c * 128:(c + 1) * 128], b3[:, k, :]) for k in range(NKC)]),
                               reads=[tok] + xall, writes=[("ps", by)])
                            col = (j - 6) * 512 if j < 8 else 1024 + (j - 14) * 512
                            s = cnt["stg16"] % 2
                            cnt["stg16"] += 1
                            dst_tok[0] = ("stg16", s)
                            evac(by, stg16[s][:], c % 2 == 0)
                            op("pool", lambda s=s, col=col, c=c: nc.gpsimd.dma_start(out=VT[t0 + c * 128:t0 + (c + 1) * 128, col:col + 512],
                                                                                      in_=stg16[s][:]),
                               reads=[("stg16", s)], writes=[("VT", t)], dkey=("st16", s))

            for t in (range(NT) if which == "C" else []):
                first = False
                t0 = t * TT
                Rall = [("R", k) for k in range(NKC)]
                mixall = [("a", k) for k in range(NKC)]
                X13 = X1T.rearrange("(k p) n -> p k n", p=128)
                MIX3 = MIX.rearrange("(k p) n -> p k n", p=128)
                if t == 0:
                    op("pool", lambda: _dsplit(nc.gpsimd, aT[:, 0:NKC, :], MIX3[:, :, 0:TT], NKC), reads=["MIX"], writes=mixall, dkey="ldmix")
                    r_load(X13, 0, [("X1T", 0)], "ldx1")
                for j in range(4):
                    b3, tok = wblock("out", first, j, [(0, 512, wout3[:, :, j * 512:(j + 1) * 512])])
                    for c in range(4):
                        dc = j * 4 + c
                        by = dc % 6
                        op("pe", mm_group(ps[by][:], [(b3[:, k, c * 128:(c + 1) * 128], aT[:, k, :]) for k in range(NKC)]),
                           reads=[tok] + mixall, writes=[("ps", by)])
                        epilogue(dc, by, ALPHA)
                layer_norm(1, LN_EPS)
                ffn("2", first)
                nxt_fn = None
                if t + 1 < NT:
                    t1n = t0 + TT
                    op("pool", lambda: _dsplit(nc.gpsimd, aT[:, 0:NKC, :], MIX3[:, :, t1n:t1n + TT], NKC), reads=["MIX"], writes=mixall, dkey="ldmix")
                    def nxt_fn(dc, t1n=t1n, t=t):
                        op("pool", lambda: nc.gpsimd.dma_start(out=R[:, dc, :], in_=X13[:, dc, t1n:t1n + TT]),
                           reads=[("X1T", t + 1)], writes=[("R", dc)], dkey=("ldx1c", dc))
                layer_norm(2, 4.0 * LN_EPS, final=(t0, nxt_fn))
        row_phase("A")
        S.barrier()
        mixer_phase(nc, S, T, ps, PF, PA, VT, MIX, gat, ngs, cont, onesb, ones32, ident, maskFB, cm32, abias_d, epst, ebias_d)
        S.barrier()
        row_phase("C")
        S.finish()
    return nc


def mixer_phase(nc, S, T, ps, PF, PA, VT, MIX, gat, ngs, cont, onesb, ones32, ident, maskFB, cm32, abias_d, epst, ebias_d):
    op = S.op
    NC = T // 64
    NP = T // 128
    NSB = T // 512
    HB = 2048
    NHB = T // HB
    PFt = lambda kind, h: PF[kind * 1024 + h * 128: kind * 1024 + (h + 1) * 128, :]

    with ExitStack() as es:
      if True:
          E = es.enter_context
          sb = lambda n_, s_, d_: _sb(nc, n_, s_, d_)
          qf = E(sb("h_q", [128, HB], F32))
          zz = [E(sb("h_z%d" % i, [128, HB], F32)) for i in range(2)]
          Lb = E(sb("h_L", [128, HB], F32))
          Pb = E(sb("h_P", [128, HB], F32))
          E1 = E(sb("h_E1", [128, HB], F32))
          vt = E(sb("h_vt", [128, NP, 128], BF16))
          qg = [E(sb("h_qg%d" % i, [128, T], BF16)) for i in range(2)]
          kg = E(sb("h_kg", [128, HB], BF16))
          kgp = E(sb("h_kgp", [128, HB], BF16))
          ATm = [E(sb("h_AT%d" % i, [128, NP, 128], BF16)) for i in range(2)]
          Sbf = [E(sb("h_S%d" % i, [128, NC, 128], BF16)) for i in range(2)]
          xtmp = E(sb("h_xtmp", [128, 4, 128], BF16))
          kgt = [E(sb("h_kt%d" % i, [128, NP, 128], BF16)) for i in range(2)]
          dvec = [E(sb("h_dv%d" % i, [128, NC], F32)) for i in range(2)]
          S32 = [[E(sb("h_S32_%d%d" % (i, j), [128, 128], F32)) for j in range(2)] for i in range(2)]
          gbuf = [E(sb("h_g%d" % i, [128, 512], F32)) for i in range(2)]
          osq2 = [E(sb("h_osq%d" % i, [128, 512], F32)) for i in range(2)]
          rstd2 = [E(sb("h_rstd%d" % i, [128, 512], F32)) for i in range(2)]
          t12 = [E(sb("h_t1%d" % i, [128, 512], F32)) for i in range(2)]
          ob = [E(sb("h_ob%d" % i, [128, 512], BF16)) for i in range(2)]
          maskS = cm32[:, 256:256 + HB]

          for h in range(NHR):
              op("sp", lambda h=h: _dsplit(nc.sync, vt[:], VT[:, h * 128:(h + 1) * 128].rearrange("(n p) f -> p n f", p=128), NP),
                 reads=["VT"], writes=["vt"], dkey="h_vt")
              for hb in range(NHB):
                  c0 = hb * HB
                  op("sp", lambda h=h, c0=c0: nc.sync.dma_start(out=qf[:], in_=PFt(0, h)[:, c0:c0 + HB]), reads=["PF"], writes=["qf"], dkey="h_q")
                  for di in range(2):
                      z = zz[di]
                      lbc = gat[:, 3 * di, h:h + 1]
                      omlc = gat[:, 3 * di + 1, h:h + 1]
                      nomlc = gat[:, 3 * di + 2, h:h + 1]
                      op("sp", lambda h=h, c0=c0, di=di, z=z: nc.sync.dma_start(out=z[:], in_=PFt(1 + di, h)[:, c0:c0 + HB]),
                         reads=["PF"], writes=[("z", di)], dkey=("h_z", di))
                      op("act", lambda z=z: nc.scalar.activation(out=z[:], in_=z[:], func=AF.Sigmoid), reads=[("z", di)], writes=[("z", di)])
                      op("act", lambda z=z, omlc=omlc, lbc=lbc: nc.scalar.activation(out=Lb[:], in_=z[:], func=AF.Ln, scale=omlc, bias=lbc),
                         reads=[("z", di), "gat"], writes=["Lb"])
                      op("dve", lambda z=z, nomlc=nomlc, omlc=omlc: nc.vector.tensor_scalar(out=z[:], in0=z[:], scalar1=nomlc, scalar2=omlc,
                                                                                             op0=ALU.mult, op1=ALU.add),
                         reads=[("z", di), "gat"], writes=[("z", di)])
                      op("dve", lambda: nc.vector.tensor_tensor_scan(out=Pb[:], data0=maskS, data1=Lb[:], initial=0.0, op0=ALU.mult, op1=ALU.add),
                         reads=["Lb", "cm32"], writes=["Pb"])
                      if di == 0:
                          G, Gtok = Pb, "Pb"
                      else:
                          op("dve", lambda: nc.vector.tensor_tensor(out=Lb[:], in0=Lb[:], in1=Pb[:], op=ALU.subtract), reads=["Lb", "Pb"], writes=["Lb"])
                          op("dve", lambda: nc.vector.tensor_tensor(
                              out=Lb[:].rearrange("p (c t) -> p c t", t=64), in0=Lb[:].rearrange("p (c t) -> p c t", t=64),
                              in1=Pb[:].rearrange("p (c t) -> p c t", t=64)[:, :, 63:64].to_broadcast([128, HB // 64, 64]), op=ALU.add),
                             reads=["Lb", "Pb"], writes=["Lb"])
                          G, Gtok = Lb, "Lb"
                      op("act", lambda G=G: nc.scalar.activation(out=E1[:], in_=G[:], func=AF.Exp), reads=[Gtok], writes=["E1"])
                      op("act", lambda G=G: nc.scalar.activation(out=G[:], in_=G[:], func=AF.Exp, scale=-1.0), reads=[Gtok], writes=[Gtok])
                      op("pool", lambda di=di, c0=c0: nc.gpsimd.tensor_tensor(out=qg[di][:, c0:c0 + HB], in0=qf[:], in1=E1[:], op=ALU.mult),
                         reads=["qf", "E1"], writes=[("qg", di)])
                      dpos = 63 if di == 0 else 0
                      op("pool", lambda di=di, c0=c0, dpos=dpos: nc.gpsimd.tensor_copy(
                          out=dvec[di][:, c0 // 64:(c0 + HB) // 64], in_=E1[:].rearrange("p (c t) -> p c t", t=64)[:, :, dpos]),
                         reads=["E1"], writes=[("dvec", di)])
                      op("dve", lambda z=z, G=G: nc.vector.tensor_tensor(out=kg[:], in0=z[:], in1=G[:], op=ALU.mult),
                         reads=[("z", di), Gtok], writes=["kg"])
                      op("pool", lambda di=di, c0=c0: nc.gpsimd.tensor_tensor(
                          out=kgp[:].rearrange("p (c t) -> p c t", t=64), in0=kg[:].rearrange("p (c t) -> p c t", t=64),
                          in1=dvec[di][:, c0 // 64:(c0 + HB) // 64].unsqueeze(2).to_broadcast([128, HB // 64, 64]), op=ALU.mult),
                         reads=["kg", ("dvec", di)], writes=["kgp"])
                      for sbi in range(HB // 512):
                          n0 = (c0 + sbi * 512) // 128
                          def tr(sbi=sbi):
                              last = None
                              for p in range(4):
                                  cs = sbi * 512 + p * 128
                                  last = nc.tensor.matmul(ps[0][:, p * 128:(p + 1) * 128], lhsT=kgp[:, cs:cs + 128], rhs=ident[:], start=True, stop=True)
                              return last
                          op("pe", tr, reads=["kgp", "ident"], writes=[("ps", 0)])
                          op("act", lambda di=di, n0=n0: nc.scalar.copy(out=kgt[di][:, n0:n0 + 4, :].rearrange("p a b -> p (a b)"), in_=ps[0][:]),
                             reads=[("ps", 0)], writes=[("kgt", di)])
                          def atm(sbi=sbi, di=di, c0=c0):
                              last = None
                              for p in range(4):
                                  cs = sbi * 512 + p * 128
                                  last = nc.tensor.matmul(ps[1][:, p * 128:(p + 1) * 128], lhsT=kg[:, cs:cs + 128], rhs=qg[di][:, c0 + cs:c0 + cs + 128],
                                                          start=True, stop=True)
                              return last
                          op("pe", atm, reads=["kg", ("qg", di)], writes=[("ps", 1)])
                          if di == 0:
                              op("dve", lambda n0=n0: nc.vector.tensor_tensor(
                                  out=ATm[0][:, n0:n0 + 4, :], in0=ps[1][:].rearrange("p (a b) -> p a b", a=4),
                                  in1=maskFB[:, 0:1, :].to_broadcast([128, 4, 128]), op=ALU.mult),
                                 reads=[("ps", 1), "maskFB"], writes=[("ATm", 0)])
                          else:
                              op("dve", lambda: nc.vector.tensor_tensor(
                                  out=xtmp[:], in0=ps[1][:].rearrange("p (a b) -> p a b", a=4),
                                  in1=maskFB[:, 1:2, :].to_broadcast([128, 4, 128]), op=ALU.mult),
                                 reads=[("ps", 1), "maskFB"], writes=["xtmp"])
                              op("pool", lambda n0=n0: nc.gpsimd.tensor_tensor(out=ATm[0][:, n0:n0 + 4, :], in0=ATm[0][:, n0:n0 + 4, :], in1=xtmp[:], op=ALU.add),
                                 reads=["xtmp", ("ATm", 0)], writes=[("ATm", 0)])
              orders = [list(range(NC)), list(range(NC - 1, -1, -1))]
              curs = [0, 0]
              for di in range(2):
                  op("dve", lambda di=di: nc.vector.memset(S32[di][0][:], 0.0), writes=[("S32", di, 0)])
                  op("pool", lambda di=di, c=orders[di][0]: nc.gpsimd.memset(Sbf[di][:, c, :], 0.0), writes=[("Sbf", di)])
              kvi = 0
              for i in range(NC - 1):
                  for di in range(2):
                      c = orders[di][i]
                      nxt = orders[di][i + 1]
                      cur = curs[di]
                      kb = (2, 6, 7)[kvi % 3]
                      kvi += 1
                      r0 = (c % 2) * 64
                      op("pe", lambda di=di, c=c, r0=r0, kb=kb: nc.tensor.matmul(
                          ps[kb][:, 0:128], lhsT=kgt[di][r0:r0 + 64, c // 2, :], rhs=vt[r0:r0 + 64, c // 2, :], start=True, stop=True),
                         reads=[("kgt", di), "vt"], writes=[("ps", kb)])
                      op("dve", lambda di=di, c=c, kb=kb, cur=cur: nc.vector.scalar_tensor_tensor(
                          out=S32[di][1 - cur][:], in0=S32[di][cur][:], scalar=dvec[di][:, c:c + 1], in1=ps[kb][:, 0:128],
                          op0=ALU.mult, op1=ALU.add),
                         reads=[("S32", di, cur), ("ps", kb), ("dvec", di)], writes=[("S32", di, 1 - cur)])
                      cur = 1 - cur
                      curs[di] = cur
                      if (di == 0 and nxt == NC // 2) or (di == 1 and nxt == NC // 2 - 1):
                          op("dve", lambda di=di, cur=cur: nc.vector.tensor_scalar(out=S32[di][cur][:], in0=S32[di][cur][:], scalar1=cont[:, 0:1],
                                                                                   scalar2=None, op0=ALU.mult),
                             reads=[("S32", di, cur), "cont"], writes=[("S32", di, cur)])
                      op("act", lambda di=di, nxt=nxt, cur=cur: nc.scalar.copy(out=Sbf[di][:, nxt, :], in_=S32[di][cur][:]),
                         reads=[("S32", di, cur)], writes=[("Sbf", di)])
              def out_stages(j):
                  s = j % 2
                  bo = 3 + s
                  msb = (5, 2)[s]
                  osq_, rstd_, t1_ = osq2[s], rstd2[s], t12[s]
                  def outmm():
                      last = None
                      for p in range(4):
                          n = j * 4 + p
                          for cc in range(2):
                              c = 2 * n + cc
                              ccols = slice(p * 128 + cc * 64, p * 128 + cc * 64 + 64)
                              nc.tensor.matmul(ps[bo][:, ccols], lhsT=vt[:, n, :], rhs=ATm[0][:, n, cc * 64:(cc + 1) * 64], start=True, stop=False)
                              nc.tensor.matmul(ps[bo][:, ccols], lhsT=Sbf[0][:, c, :], rhs=qg[0][:, c * 64:(c + 1) * 64], start=False, stop=False)
                              last = nc.tensor.matmul(ps[bo][:, ccols], lhsT=Sbf[1][:, c, :], rhs=qg[1][:, c * 64:(c + 1) * 64], start=False, stop=True)
                      return last
                  def st0():
                      op("sp", lambda: nc.sync.dma_start(out=gbuf[s][:], in_=PFt(3, h)[:, j * 512:(j + 1) * 512]),
                         reads=["PF"], writes=[("gbuf", s)], dkey=("h_g", s))
                      op("pe", outmm, reads=["vt", ("ATm", 0), ("Sbf", 0), ("Sbf", 1), ("qg", 0), ("qg", 1)], writes=[("ps", bo)])
                  return [
                      st0,
                      lambda: op("act", lambda: nc.scalar.activation(out=osq_[:], in_=ps[bo][:], func=AF.Square), reads=[("ps", bo)], writes=[("osq", s)]),
                      lambda: op("pe", lambda: nc.tensor.matmul(ps[msb][:], lhsT=ones32[:], rhs=osq_[:], start=True, stop=True),
                                 reads=[("osq", s), "ones32"], writes=[("ps", msb)]),
                      lambda: op("act", lambda: nc.scalar.activation(out=rstd_[:], in_=ps[msb][:], func=AF.Sqrt, bias=epst[:, 2:3], scale=1.0),
                                 reads=[("ps", msb), "epst"], writes=[("rstd", s)]),
                      lambda: op("dve", lambda: nc.vector.reciprocal(out=rstd_[:], in_=rstd_[:]), reads=[("rstd", s)], writes=[("rstd", s)]),
                      lambda: op("act", lambda: nc.scalar.activation(out=gbuf[s][:], in_=gbuf[s][:], func=AF.Silu), reads=[("gbuf", s)], writes=[("gbuf", s)]),
                      lambda: op("dve", lambda: nc.vector.tensor_tensor(out=t1_[:], in0=ps[bo][:], in1=rstd_[:], op=ALU.mult),
                                 reads=[("ps", bo), ("rstd", s)], writes=[("t1", s)]),
                      lambda: op("pool", lambda: nc.gpsimd.tensor_tensor(out=t1_[:], in0=t1_[:], in1=gbuf[s][:], op=ALU.mult),
                                 reads=[("t1", s), ("gbuf", s)], writes=[("t1", s)]),
                      lambda: op("act", lambda: nc.scalar.activation(out=ob[s][:], in_=t1_[:], func=AF.Identity, scale=ngs[:, h:h + 1]),
                                 reads=[("t1", s), "ngs"], writes=[("ob", s)]),
                      lambda: op("pool", lambda: nc.gpsimd.dma_start(out=MIX[h * 128:(h + 1) * 128, j * 512:(j + 1) * 512], in_=ob[s][:]),
                                 reads=[("ob", s)], writes=["MIX"], dkey=("h_ob", s)),
                  ]
              for j0 in range(0, NSB, 2):
                  stA = out_stages(j0)
                  stB = out_stages(j0 + 1) if j0 + 1 < NSB else None
                  for k in range(len(stA)):
                      stA[k]()
                      if stB is not None:
                          stB[k]()

    S.barrier()
    with ExitStack() as es:
        E = es.enter_context
        sb = lambda n_, s_, d_: _sb(nc, n_, s_, d_)
        PAD = 1024
        qa = E(sb("a_q", [128, T], BF16))
        kp = E(sb("a_k", [128, PAD + T + PAD], BF16))
        abias = E(sb("a_bias", [128, 24, 256], F32))
        NUM = E(sb("a_num", [128, T], F32))
        DEN = E(sb("a_den", [128, T], F32))
        vs = E(sb("a_vs", [128, 48 * T // 4096 + 16, 128], BF16))
        sc = [E(sb("a_sc%d" % i, [128, 256], F32)) for i in range(2)]
        PT = [E(sb("a_pt%d" % i, [128, 256], BF16)) for i in range(3)]
        attb = E(sb("a_out", [128, T], BF16))
        ebias = E(sb("a_ebias", [128, 24, 256], F32))
        ncont = E(sb("a_ncont", [128, 1], F32))
        PTe = E(sb("a_pte", [128, 128], BF16))
        sce = E(sb("a_sce", [128, 128], F32))
        EE = E(sb("a_ee", [128, 128], BF16))
        PTd = E(sb("a_ptd", [128, 128], BF16))
        op("sp", lambda: nc.sync.dma_start(out=abias[:].rearrange("p a b -> p (a b)"), in_=abias_d), writes=["abias"], dkey="a_bias")
        op("sp", lambda: nc.sync.dma_start(out=ebias[:].rearrange("p a b -> p (a b)"), in_=ebias_d), writes=["ebias"], dkey="a_ebias")
        op("dve", lambda: nc.vector.tensor_scalar(out=ncont[:], in0=cont[:], scalar1=-1.0, scalar2=1.0, op0=ALU.mult, op1=ALU.add),
           reads=["cont"], writes=["ncont"])
        op("pool", lambda: nc.gpsimd.memset(kp[:, 0:PAD], 0.0), writes=[("kp", 0)])
        op("pool", lambda: nc.gpsimd.memset(kp[:, PAD + T:PAD + T + PAD], 0.0), writes=[("kp", 0)])
        scale = float(HD ** -0.5)
        qaB = E(sb("a_qB", [128, T], BF16))
        kpB = E(sb("a_kB", [128, PAD + T + PAD], BF16))
        vsB = E(sb("a_vsB", [128, 48 * T // 4096 + 16, 128], BF16))
        PT.append(E(sb("a_pt3", [128, 256], BF16)))
        sc.append(E(sb("a_sc2", [128, 256], F32)))
        op("pool", lambda: nc.gpsimd.memset(kpB[:, 0:PAD], 0.0), writes=[("kp", 1)])
        op("pool", lambda: nc.gpsimd.memset(kpB[:, PAD + T:PAD + T + PAD], 0.0), writes=[("kp", 1)])
        qa2, kp2, vs2 = [qa, qaB], [kp, kpB], [vs, vsB]
        mb2 = [E(sb("a_mb%d" % i, [128, 3, 256], F32)) for i in range(2)]
        EEn2 = [E(sb("a_een%d" % i, [128, 3, 128], BF16)) for i in range(2)]
        mtmp = E(sb("a_mtmp", [128, 3, 128], F32))
        negn = E(sb("a_negn", [128, 1], F32))
        op("dve", lambda: nc.vector.tensor_scalar(out=negn[:], in0=ncont[:], scalar1=NEG, scalar2=None, op0=ALU.mult), reads=["ncont"], writes=["negn"])
        NPT, NSC = len(PT), len(sc)

        def load_head(h):
            hp = h % 2
            op("sp", lambda: nc.sync.dma_start(out=qa2[hp][:], in_=PA[h * 128:(h + 1) * 128, :]), reads=["PA"], writes=[("qa", hp)], dkey=("a_q", hp))
            op("sp", lambda: nc.sync.dma_start(out=kp2[hp][:, PAD:PAD + T], in_=PA[1024 + h * 128:1024 + (h + 1) * 128, :]),
               reads=["PA"], writes=[("kp", hp)], dkey=("a_k", hp))

        def load_v(h, pi, vp):
            d = PATTERNS[pi]
            L = T // d
            nq = L // 128
            nk = nq + 1
            vsrc = VT[:, 1024 + h * 128:1024 + (h + 1) * 128]
            v4 = vs2[vp][:, 0:d * nk, :].rearrange("p (r k) f -> p r k f", r=d)
            def ldv():
                outs = []
                vr = vsrc.rearrange("(i r) f -> r i f", r=d)
                for r in range(d):
                    outs.append(nc.sync.dma_start(out=v4[64:128, r, 0, :], in_=vr[r, 0:64, :]))
                    outs.append(nc.sync.dma_start(out=v4[0:64, r, nq, :], in_=vr[r, L - 64:L, :]))
                    if nq > 1:
                        outs += _dsplit(nc.sync, v4[:, r, 1:nq, :], vr[r, 64:L - 64, :].rearrange("(k a) f -> a k f", a=128), nq - 1)
                return outs
            op("sp", ldv, reads=["VT"], writes=[("vs", vp)], dkey=("a_vs", vp))

        seq = [(h, pi) for h in range(NHR) for pi in range(3)]
        jobs = []
        kcnt = 0
        for si, (h, pi) in enumerate(seq):
            d = PATTERNS[pi]
            hp = h % 2
            vp = si % 2
            L = T // d
            nq = L // 128
            nk = nq + 1
            qaT, kpT, vsT = qa2[hp], kp2[hp], vs2[vp]
            qn = 0
            pend = []
            first_tile = True
            for r in range(d):
                for kt in range(nk):
                    b0 = 128 if kt == 0 else 0
                    b1 = 128 if kt == nq else 256
                    ks = PAD + (128 * kt - 64) * d + r
                    qs = (128 * (kt - 1) + b0) * d + r
                    nqc = b1 - b0
                    sbk = 6 + kcnt % 2
                    s2 = kcnt % NSC
                    s3 = kcnt % NPT
                    kcnt += 1
                    lhs = kpT[:, _ss(ks, 128, d)]
                    rhs = qaT[:, _ss(qs, nqc, d)]
                    mid = (nq % 2 == 0 and kt == nq // 2)
                    ph = pi * 8 + h
                    if mid:
                        btile, bidx = mb2[hp], pi
                    else:
                        btile, bidx = (ebias if (kt == 0 or kt == nq) else abias), ph
                    pre = None
                    tile_i = r * nk + kt
                    if tile_i == 0 and pi == 0:
                        def pre(h=h, hp=hp):
                            H0, H1 = slice(0, 64), slice(64, 128)
                            hs = _ss(h, 3, 8)
                            mbT, EEnT = mb2[hp], EEn2[hp]
                            tk = ("mb", hp)
                            op("act", lambda: nc.scalar.copy(out=mbT[H0, :, 0:128], in_=abias[H0, hs, 0:128]), reads=["abias"], writes=[tk])
                            op("dve", lambda: nc.vector.tensor_scalar(out=mbT[H1, :, 0:128], in0=abias[H1, hs, 0:128], scalar1=negn[H1, 0:1], scalar2=None,
                                                                      op0=ALU.add), reads=["abias", "negn"], writes=[tk])
                            op("dve", lambda: nc.vector.tensor_scalar(out=mbT[H0, :, 128:256], in0=abias[H0, hs, 128:256], scalar1=negn[H0, 0:1], scalar2=None,
                                                                      op0=ALU.add), reads=["abias", "negn"], writes=[tk])
                            op("dve", lambda: nc.vector.tensor_scalar(out=mtmp[H1, :, :], in0=ebias[H1, hs, 128:256], scalar1=ncont[H1, 0:1], scalar2=None,
                                                                      op0=ALU.mult), reads=["ebias", "ncont"], writes=["mtmp"])
                            op("dve", lambda: nc.vector.scalar_tensor_tensor(out=mbT[H1, :, 128:256], in0=abias[H1, hs, 128:256], scalar=cont[H1, 0:1],
                                                                             in1=mtmp[H1, :, :], op0=ALU.mult, op1=ALU.add),
                               reads=["abias", "cont", "mtmp"], writes=[tk])
                            op("act", lambda: nc.scalar.activation(out=mtmp[H1, :, :], in_=ebias[H1, hs, 0:128], func=AF.Exp), reads=["ebias", "mtmp"], writes=["mtmp"])
                            op("dve", lambda: nc.vector.tensor_scalar(out=EEnT[H1, :, :], in0=mtmp[H1, :, :], scalar1=ncont[H1, 0:1], scalar2=None,
                                                                      op0=ALU.mult), reads=["mtmp", "ncont"], writes=[tk])
                    if tile_i == 3:
                        nxt = seq[si + 1] if si + 1 < len(seq) else None
                        def pre(nxt=nxt, si=si, h=h, pi=pi):
                            if nxt is not None:
                                if nxt[1] == 1 and nxt[0] + 1 < NHR:
                                    load_head(nxt[0] + 1)
                                load_v(nxt[0], nxt[1], (si + 1) % 2)

                    def s1(pre=pre, sbk=sbk, s2=s2, s3=s3, b0=b0, b1=b1, lhs=lhs, rhs=rhs, mid=mid, ph=ph, btile=btile, bidx=bidx, hp=hp, hook=None):
                        if pre is not None:
                            pre()
                        op("pe", lambda: nc.tensor.matmul(ps[sbk][:, b0:b1], lhsT=lhs, rhs=rhs, start=True, stop=True),
                           reads=[("kp", hp), ("qa", hp)], writes=[("ps", sbk)])
                        op("dve", lambda: nc.vector.scalar_tensor_tensor(
                            out=sc[s2][:, b0:b1], in0=ps[sbk][:, b0:b1], scalar=scale, in1=btile[:, bidx, b0:b1], op0=ALU.mult, op1=ALU.add),
                           reads=[("ps", sbk), "abias", "ebias", ("mb", hp)], writes=[("sc", s2)])
                        op("act", lambda: nc.scalar.activation(out=PT[s3][:, b0:b1], in_=sc[s2][:, b0:b1], func=AF.Exp),
                           reads=[("sc", s2)], writes=[("PT", s3)])

                    vidx = r * nk + kt
                    evac = None
                    if kt >= 1:
                        pend.append((r, kt - 1, (qn - 1) % 4))
                        if (qn - 1) % 4 == 3:
                            bsel = ((qn - 1) // 4) % 2
                            groups = []
                            for (rr, kq, sl) in pend:
                                if groups and groups[-1][0] == rr and groups[-1][1] + groups[-1][3] == kq:
                                    groups[-1][3] += 1
                                else:
                                    groups.append([rr, kq, sl, 1])
                            evac = (bsel, groups)
                            pend = []
                    last_of_head = (pi == 2 and r == d - 1 and kt == nk - 1)

                    def s2f(kt=kt, s3=s3, vidx=vidx, qn=qn, nq=nq, mid=mid, vsT=vsT, vp=vp, evac=evac, d=d, pi=pi, last_of_head=last_of_head, h=h, hp=hp):
                        def pv():
                            last = None
                            if kt >= 1:
                                q = qn - 1
                                nb, db, col = 2 + (q // 4) % 2, 4 + (q // 4) % 2, (q % 4) * 128
                                rows = slice(0, 64) if kt == nq else slice(0, 128)
                                nc.tensor.matmul(ps[nb][:, col:col + 128], lhsT=vsT[rows, vidx, :], rhs=PT[s3][rows, 0:128], start=False, stop=True)
                                last = nc.tensor.matmul(ps[db][:, col:col + 128], lhsT=onesb[:, :], rhs=PT[s3][:, 0:128], start=False, stop=(not mid))
                                if mid:
                                    last = nc.tensor.matmul(ps[db][:, col:col + 128], lhsT=onesb[64:128, :], rhs=EEn2[hp][64:128, pi, :], start=False, stop=True)
                            if kt <= nq - 1:
                                q = qn
                                nb, db, col = 2 + (q // 4) % 2, 4 + (q // 4) % 2, (q % 4) * 128
                                rows = slice(64, 128) if kt == 0 else slice(0, 128)
                                nc.tensor.matmul(ps[nb][:, col:col + 128], lhsT=vsT[rows, vidx, :], rhs=PT[s3][rows, 128:256], start=True, stop=False)
                                last = nc.tensor.matmul(ps[db][:, col:col + 128], lhsT=onesb[rows, :], rhs=PT[s3][rows, 128:256], start=True, stop=False)
                            return last
                        wr = set()
                        if kt >= 1:
                            wr.add(((qn - 1) // 4) % 2)
                        if kt <= nq - 1:
                            wr.add((qn // 4) % 2)
                        op("pe", pv, reads=[("PT", s3), ("vs", vp), "onesb"] + ([("mb", hp)] if mid else []),
                           writes=[("ps", 2 + w) for w in wr] + [("ps", 4 + w) for w in wr])
                        if evac is not None:
                            bsel, groups = evac
                            for (rr, kq, sl, n) in groups:
                                a0 = rr + d * 128 * kq
                                for (bank, dst, tokn) in ((2 + bsel, NUM, "NUM"), (4 + bsel, DEN, "DEN")):
                                    dv = dst[:, _ss(a0, 128 * n, d)]
                                    src_ = ps[bank][:, sl * 128:(sl + n) * 128]
                                    if pi == 0:
                                        op("act", lambda dv=dv, src_=src_: nc.scalar.copy(out=dv, in_=src_), reads=[("ps", bank)], writes=[tokn])
                                    else:
                                        op("dve", lambda dv=dv, src_=src_: nc.vector.tensor_tensor(out=dv, in0=dv, in1=src_, op=ALU.add),
                                           reads=[("ps", bank), tokn], writes=[tokn])
                        if last_of_head:
                            op("act", lambda: nc.scalar.activation(out=DEN[:], in_=DEN[:], func=AF.Ln), reads=["DEN"], writes=["DEN"])
                            op("act", lambda: nc.scalar.activation(out=DEN[:], in_=DEN[:], func=AF.Exp, scale=-1.0), reads=["DEN"], writes=["DEN"])
                            op("pool", lambda: nc.gpsimd.tensor_tensor(out=attb[:], in0=NUM[:], in1=DEN[:], op=ALU.mult), reads=["NUM", "DEN"], writes=["attb"])
                            op("pool", lambda: nc.gpsimd.dma_start(out=MIX[1024 + h * 128:1024 + (h + 1) * 128, :], in_=attb[:]),
                               reads=["attb"], writes=["MIX"], dkey="a_out")
                    jobs.append((s1, s2f))
                    if kt <= nq - 1:
                        qn += 1
        load_head(0)
        load_v(0, 0, 0)
        LOOK = 2
        for i in range(len(jobs) + LOOK):
            if i < len(jobs):
                jobs[i][0]()
            if i >= LOOK:
                jobs[i - LOOK][1]()


def _consts():
    slopes = 2.0 ** (-8.0 * (np.arange(NH, dtype=np.float64) + 1.0) / NH)
    a = np.arange(128)[:, None]
    b = np.arange(256)[None, :]
    rel = a - b + 64
    ab = np.zeros((128, 24, 256), np.float32)
    for pi, d in enumerate(PATTERNS):
        for h in range(NH):
            v = -slopes[h] * d * np.abs(rel)
            v = np.where(np.abs(rel) <= 64, v, NEG)
            ab[:, pi * 8 + h, :] = v.astype(np.float32)
    eb = ab.copy()
    m_edge = ((a >= 64) & (b >= 128) & (b < 192) & (rel < 0)) | ((a >= 64) & (b == a))
    eb[np.broadcast_to(m_edge[:, None, :], eb.shape)] = NEG
    s = np.arange(128)[:, None]
    t = np.arange(128)[None, :]
    same = (s // 64) == (t // 64)
    mF = (same & (s <= t)).astype(np.float32)
    mB = (same & (s >= t)).astype(np.float32)
    scan = np.ones((128, 2048), np.float32)
    scan[:, ::64] = 0.0
    cmask = np.concatenate([mF, mB, scan], axis=1)
    return ab.reshape(128, 24 * 256), np.ascontiguousarray(cmask), np.ascontiguousarray(eb.reshape(128, 24 * 256))


def _pc(v):
    return np.ascontiguousarray(np.asarray(v, np.float32).reshape(-1, 128).T)


_NC_CACHE = {}


def run_cores(slots, conts, W, T, debug=False):
    key = (T, debug)
    if key not in _NC_CACHE:
        _NC_CACHE[key] = build_program(T, debug)
    nc = _NC_CACHE[key]
    ab, cmask, eb = _consts()
    lnp = np.concatenate([_pc(W[k][0]) for k in ("ln1_g", "ln1_b", "ln2_g", "ln2_b", "ln3_g", "ln3_b")], axis=1)
    hgp = np.concatenate([_pc(W["hgrn_lb_fwd"][0]), _pc(W["hgrn_lb_fwd"][1]), _pc(W["hgrn_lb_bwd"][0]), _pc(W["hgrn_lb_bwd"][1]),
                          _pc(W["hgrn_norm_g"][0])], axis=1)
    common = {
        "wg1": np.ascontiguousarray(W["ffn1_w_gate"][0]), "wu1": np.ascontiguousarray(W["ffn1_w_up"][0]),
        "wd1": np.ascontiguousarray(W["ffn1_w_down"][0]),
        "wg2": np.ascontiguousarray(W["ffn2_w_gate"][0]), "wu2": np.ascontiguousarray(W["ffn2_w_up"][0]),
        "wd2": np.ascontiguousarray(W["ffn2_w_down"][0]),
        "w_in": np.ascontiguousarray(W["w_in"][0]), "w_out": np.ascontiguousarray(W["w_out"][0]),
        "lnp": np.ascontiguousarray(lnp), "hgp": np.ascontiguousarray(hgp), "abias": ab, "cmask": cmask, "ebias": eb,
    }
    in_maps = []
    for x, c in zip(slots, conts):
        m = dict(common)
        m["xT"] = np.ascontiguousarray(np.asarray(x, np.float32).T)
        m["cont"] = np.full((128, 1), c, np.float32)
        in_maps.append(m)
    res = run_bass_kernel_spmd(nc, in_maps, core_ids=list(range(len(slots))))
    return res.results


def kernel(**inputs):
    W = {k: np.asarray(v) for k, v in inputs.items()}
    xp = W["x_prompt"]
    xs = W["x_sample"]
    T = 4096
    slots = [xp[b] for b in range(4)]
    pairs = [np.concatenate([xs[0], xs[1]], axis=0), np.concatenate([xs[2], xs[3]], axis=0)]
    slots += pairs + pairs
    conts = [1.0] * 4 + [0.0] * 4
    res = run_cores(slots, conts, W, T)
    yp = np.stack([np.ascontiguousarray(res[b]["yT"].T) for b in range(4)], axis=0).astype(np.float32)
    ys = []
    for i in range(2):
        y = np.ascontiguousarray(res[4 + i]["yT"].T)
        ys.append(y[:2048])
        ys.append(y[2048:])
    ys = np.stack(ys, axis=0).astype(np.float32)
    return (yp, ys)
```
